# Optimizing a Trainium2 kernel written in Bass

```python
import math
import jax, jax.numpy as jnp
from jax import lax
import numpy as np

D_MODEL = 1024
BATCH = 16
SEQ = 4096
DEPTH = 2

ATT_HEADS = 8
ATT_KV_HEADS = 2
ATT_HEAD_DIM = 64
ATT_GROUPS = ATT_HEADS // ATT_KV_HEADS
WINDOW = 128
ATT_BLOCK = 128
N_BUCKETS = 32
MAX_DISTANCE = 128
HG_HEADS = 4
HG_DK = 128
HG_DV = 128
HG_CHUNK = 64
D_FF = 2816
CONV_WIDTH = 3
EPS = 1e-6

ATT_Q = ATT_HEADS * ATT_HEAD_DIM
ATT_KV = ATT_KV_HEADS * ATT_HEAD_DIM
HG_K = HG_HEADS * HG_DK
HG_V = HG_HEADS * HG_DV
IN_SPLITS = (ATT_Q, ATT_KV, ATT_KV, HG_K, HG_K, HG_V, HG_V, D_MODEL, D_MODEL)
D_IN = ATT_Q + 2 * ATT_KV + 2 * HG_K + 2 * HG_V + 2 * D_MODEL

kernel_name = "hybrid_swa_hgrn2_gated_merge"


def rmsnorm(x, g):
    xf = x.astype(jnp.float32)
    y = xf * lax.rsqrt(jnp.mean(xf * xf, axis=-1, keepdims=True) + EPS)
    return (y * g.astype(jnp.float32)).astype(x.dtype)


def t5_bucket(dist):
    max_exact = N_BUCKETS // 2
    is_small = dist < max_exact
    d = jnp.maximum(dist, 1).astype(jnp.float32)
    large = max_exact + (jnp.log(d / max_exact) / math.log(MAX_DISTANCE / max_exact)
                         * (N_BUCKETS - max_exact)).astype(jnp.int32)
    large = jnp.minimum(large, N_BUCKETS - 1)
    return jnp.where(is_small, dist, large)


def swa_attention(q, k, v, sinks, rel_bias):
    B, S = q.shape[0], q.shape[1]
    nb = S // ATT_BLOCK
    qb = q.reshape(B, nb, ATT_BLOCK, ATT_KV_HEADS, ATT_GROUPS, ATT_HEAD_DIM)

    def band(t):
        tp = jnp.pad(t, ((0, 0), (ATT_BLOCK, 0), (0, 0), (0, 0)))
        prev = tp[:, :S].reshape(B, nb, ATT_BLOCK, ATT_KV_HEADS, ATT_HEAD_DIM)
        cur = t.reshape(B, nb, ATT_BLOCK, ATT_KV_HEADS, ATT_HEAD_DIM)
        return jnp.concatenate([prev, cur], axis=2)

    kb, vb = band(k), band(v)
    scale = ATT_HEAD_DIM ** -0.5
    scores = jnp.einsum('bnqhgd,bnkhd->bnhgqk', qb, kb).astype(jnp.float32) * scale

    qi = jnp.arange(ATT_BLOCK)[:, None] + ATT_BLOCK
    kj = jnp.arange(2 * ATT_BLOCK)[None, :]
    dist = qi - kj
    kpos = (jnp.arange(nb)[:, None] - 1) * ATT_BLOCK + jnp.arange(2 * ATT_BLOCK)[None, :]
    valid = ((dist >= 0) & (dist < WINDOW))[None] & (kpos >= 0)[:, None, :]
    bias = rel_bias.astype(jnp.float32)[t5_bucket(jnp.maximum(dist, 0))]
    bias = bias.transpose(2, 0, 1).reshape(ATT_KV_HEADS, ATT_GROUPS, ATT_BLOCK, 2 * ATT_BLOCK)
    scores = jnp.where(valid[None, :, None, None], scores + bias, -jnp.inf)

    sink = sinks.astype(jnp.float32).reshape(ATT_KV_HEADS, ATT_GROUPS, 1, 1)
    m = jnp.maximum(jnp.max(scores, axis=-1, keepdims=True), sink)
    p = jnp.exp(scores - m)
    denom = jnp.sum(p, axis=-1, keepdims=True) + jnp.exp(sink - m)
    probs = (p / denom).astype(v.dtype)
    out = jnp.einsum('bnhgqk,bnkhd->bnqhgd', probs, vb)
    return out.reshape(B, S, ATT_Q)


def hgrn2(q, f_pre, i, lb):
    B, S = q.shape[0], q.shape[1]
    nc = S // HG_CHUNK
    lbf = lb.astype(jnp.float32)
    f = lbf + (1.0 - lbf) * jax.nn.sigmoid(f_pre.astype(jnp.float32))
    logf = jnp.log(f)
    kk = 1.0 - f
    qq = jax.nn.silu(q.astype(jnp.float32))
    vv = i.astype(jnp.float32)

    def chunks(t):
        return t.reshape(B, nc, HG_CHUNK, HG_HEADS, t.shape[-1]).transpose(1, 0, 3, 2, 4)

    causal = jnp.tril(jnp.ones((HG_CHUNK, HG_CHUNK), dtype=bool))

    def step(state, inp):
        qc, kc, vc, gc = inp
        b = jnp.cumsum(gc, axis=2)
        diff = b[:, :, :, None, :] - b[:, :, None, :, :]
        decay = jnp.exp(jnp.where(causal[:, :, None], diff, -jnp.inf))
        att = jnp.einsum('bhtk,bhsk,bhtsk->bhts', qc, kc, decay)
        o = jnp.einsum('bhts,bhsv->bhtv', att, vc) + jnp.einsum('bhtk,bhkv->bhtv', qc * jnp.exp(b), state)
        b_last = b[:, :, -1:, :]
        state = jnp.exp(b_last[:, :, 0, :])[..., None] * state + \
            jnp.einsum('bhsk,bhsv->bhkv', kc * jnp.exp(b_last - b), vc)
        return state, o

    s0 = jnp.zeros((B, HG_HEADS, HG_DK, HG_DV), jnp.float32)
    _, o = lax.scan(step, s0, (chunks(qq), chunks(kk), chunks(vv), chunks(logf)))
    o = o.transpose(1, 0, 3, 2, 4).reshape(B, S, HG_HEADS, HG_DV)
    return o


def causal_dwconv(u, w, b):
    C = u.shape[-1]
    y = lax.conv_general_dilated(u, w[:, None, :].astype(u.dtype), window_strides=(1,),
                                 padding=[(CONV_WIDTH - 1, 0)],
                                 dimension_numbers=('NWC', 'WIO', 'NWC'),
                                 feature_group_count=C)
    return y + b.astype(u.dtype)


def setup_inputs(seed: int = 0) -> dict:
    key = jax.random.key(seed)
    ks = jax.random.split(key, 20)
    nrm = lambda k, shape, s: jax.random.normal(k, shape, jnp.float32) * s
    return {
        "x": nrm(ks[0], (BATCH, SEQ, D_MODEL), 1.0),
        "norm1": 1.0 + nrm(ks[1], (DEPTH, D_MODEL), 0.02),
        "w_in": nrm(ks[2], (DEPTH, D_MODEL, D_IN), D_MODEL ** -0.5),
        "q_norm": 1.0 + nrm(ks[3], (DEPTH, ATT_HEAD_DIM), 0.02),
        "k_norm": 1.0 + nrm(ks[4], (DEPTH, ATT_HEAD_DIM), 0.02),
        "sinks": nrm(ks[5], (DEPTH, ATT_HEADS), 0.5),
        "rel_bias": nrm(ks[6], (N_BUCKETS, ATT_HEADS), 0.1),
        "hg_lb": nrm(ks[7], (DEPTH, HG_K), 0.1),
        "hg_norm": 1.0 + nrm(ks[8], (DEPTH, HG_DV), 0.02),
        "w_pa": nrm(ks[9], (DEPTH, ATT_Q, D_MODEL), ATT_Q ** -0.5),
        "w_ph": nrm(ks[10], (DEPTH, HG_V, D_MODEL), HG_V ** -0.5),
        "w_out": nrm(ks[11], (DEPTH, D_MODEL, D_MODEL), D_MODEL ** -0.5),
        "norm2": 1.0 + nrm(ks[12], (DEPTH, D_MODEL), 0.02),
        "w_up": nrm(ks[13], (DEPTH, D_MODEL, 2 * D_FF), D_MODEL ** -0.5),
        "conv_w": nrm(ks[14], (DEPTH, CONV_WIDTH, 2 * D_FF), CONV_WIDTH ** -0.5),
        "conv_b": nrm(ks[15], (DEPTH, 2 * D_FF), 0.01),
        "w_down": nrm(ks[16], (DEPTH, D_FF, D_MODEL), D_FF ** -0.5),
    }


def reference(x, norm1, w_in, q_norm, k_norm, sinks, rel_bias, hg_lb, hg_norm,
              w_pa, w_ph, w_out, norm2, w_up, conv_w, conv_b, w_down):
    B, S = x.shape[0], x.shape[1]
    offs = tuple(int(o) for o in np.cumsum(IN_SPLITS)[:-1])
    p_lb = jax.nn.softmax(hg_lb.astype(jnp.float32), axis=0)
    lbs = jnp.cumsum(p_lb, axis=0) - p_lb[0:1]

    for l in range(DEPTH):
        h = rmsnorm(x, norm1[l])
        proj = h @ w_in[l].astype(h.dtype)
        q_a, k_a, v_a, q_r, f_r, i_r, g_r, gate_a, gate_r = jnp.split(proj, offs, axis=-1)

        q_a = rmsnorm(q_a.reshape(B, S, ATT_HEADS, ATT_HEAD_DIM), q_norm[l])
        k_a = rmsnorm(k_a.reshape(B, S, ATT_KV_HEADS, ATT_HEAD_DIM), k_norm[l])
        v_a = v_a.reshape(B, S, ATT_KV_HEADS, ATT_HEAD_DIM)
        a_out = swa_attention(q_a, k_a, v_a, sinks[l], rel_bias)

        r = hgrn2(q_r.reshape(B, S, HG_HEADS, HG_DK),
                  f_r.reshape(B, S, HG_HEADS, HG_DK),
                  i_r.reshape(B, S, HG_HEADS, HG_DV),
                  lbs[l].reshape(HG_HEADS, HG_DK))
        r = rmsnorm(r, hg_norm[l]).astype(x.dtype)
        r_out = (r * jax.nn.silu(g_r.reshape(B, S, HG_HEADS, HG_DV))).reshape(B, S, HG_V)

        merged = jax.nn.sigmoid(gate_a) * (a_out @ w_pa[l].astype(x.dtype)) + \
            jax.nn.sigmoid(gate_r) * (r_out @ w_ph[l].astype(x.dtype))
        x = x + merged @ w_out[l].astype(x.dtype)

        h2 = rmsnorm(x, norm2[l])
        u = causal_dwconv(h2 @ w_up[l].astype(x.dtype), conv_w[l], conv_b[l])
        u_gate, u_val = jnp.split(u, 2, axis=-1)
        x = x + (jax.nn.silu(u_gate) * u_val) @ w_down[l].astype(x.dtype)
    return x
```

```python
from contextlib import ExitStack
import math
import numpy as np
import concourse.bass as bass
import concourse.mybir as mybir
from concourse.bass_utils import run_bass_kernel_spmd

F32 = mybir.dt.float32
BF16 = mybir.dt.bfloat16
ALU = mybir.AluOpType
AF = mybir.ActivationFunctionType
AX = mybir.AxisListType

D = 1024
DIN = 4864
DFF = 2816
NUP = 5632
T = 512
NSUB = 4
EPS = 1e-6
NSLOT = 4
FILL_START = 0
NEG = -30000.0
ENG = ('pe', 'act', 'dve', 'pool', 'sp')
HOP = 0.25
DEFC = {'pe': 0.22, 'act': 0.65, 'dve': 0.65, 'pool': 1.15, 'sp': 0.1}


def t5_ranges():
    d = np.arange(0, 128)
    dd = np.maximum(d, 1).astype(np.float32)
    large = 16 + (np.log(dd / np.float32(16)) / np.float32(math.log(128 / 16)) * np.float32(16)).astype(np.int32)
    large = np.minimum(large, 31)
    b = np.where(d < 16, d, large)
    return b


class Buf:
    __slots__ = ('w', 'r', 'const')

    def __init__(self, const=False):
        self.w = None
        self.r = {}
        self.const = const


class Ctx:
    def __init__(self, nc, stack):
        self.nc = nc
        self.stack = stack
        self.streams = {k: [] for k in ENG}
        self.sems = {}
        self.cnt = {}
        self.seen = {k: {} for k in ENG}
        for k in ENG:
            self.sems[k] = stack.enter_context(nc.semaphore("sem_" + k))
            self.cnt[k] = 0
        self.efree = {k: 0.0 for k in ENG}
        self.fin = {}
        self.step_end = 0.0
        self.nops = 0
        self.snap = {}
        self.nwait = 0

    def _ready_t(self, eng, R, W):
        t = 0.0
        for b in R:
            if b.w is not None:
                t = max(t, self.fin.get(b.w, 0.0) + (0.0 if b.w[0] == eng else HOP))
        for b in W:
            if b.w is not None:
                t = max(t, self.fin.get(b.w, 0.0) + (0.0 if b.w[0] == eng else HOP))
            for sv in b.r.items():
                t = max(t, self.fin.get(sv, 0.0) + (0.0 if sv[0] == eng else HOP))
        return t

    def _sem(self, name):
        if name not in self.sems:
            self.sems[name] = self.stack.enter_context(self.nc.semaphore("sem_" + name))
            self.cnt[name] = 0
        return name

    def _waits(self, eng, R, W, selfdep=True):
        need = {}
        for b in R:
            if b.w is not None:
                need[b.w[0]] = max(need.get(b.w[0], 0), b.w[1])
        for b in W:
            if b.w is not None:
                need[b.w[0]] = max(need.get(b.w[0], 0), b.w[1])
            for s, v in b.r.items():
                need[s] = max(need.get(s, 0), v)
        seen = self.seen[eng]
        for s, v in sorted(need.items(), key=lambda kv: -self.fin.get(kv, 0.0)):
            if s == eng and not selfdep:
                continue
            if seen.get(s, 0) < v:
                self.streams[eng].append(('w', s, v))
                self.nwait += 1
                seen[s] = v
                sn = self.snap.get((s, v))
                if sn:
                    for s2, v2 in sn.items():
                        if seen.get(s2, 0) < v2:
                            seen[s2] = v2

    def _mark(self, tk, R, W):
        for b in R:
            if not b.const:
                b.r[tk[0]] = max(b.r.get(tk[0], 0), tk[1])
        for b in W:
            b.w = tk
            b.r = {}

    def op(self, eng, fns, R=(), W=(), selfdep=True):
        rt = self._ready_t(eng, R, W)
        self._waits(eng, R, W, selfdep)
        if callable(fns):
            fns = [fns]
        self.cnt[eng] += 1
        tk = (eng, self.cnt[eng])
        end = max(rt, self.efree[eng]) + sum(getattr(f, 'c', DEFC[eng]) for f in fns) * (1.8 if eng == 'pool' else 1.0)
        self.efree[eng] = end
        self.fin[tk] = end
        self.step_end = max(self.step_end, end)
        self.nops += 1
        self.streams[eng].append(('o', fns, eng, 1))
        self.snap[tk] = dict(self.seen[eng])
        self._mark(tk, R, W)

    def dma(self, eng, out, in_, R=(), W=(), sem='d0', **kw):
        if sem.startswith('cst'):
            self.nuniq = getattr(self, 'nuniq', 0) + 1
            sem = 'cstu%d' % self.nuniq
        self._sem(sem)
        rt = self._ready_t(eng, R, W)
        self._waits(eng, R, W)
        self.cnt[sem] += 16
        tk = (sem, self.cnt[sem])
        st_ = max(rt, self.efree[eng])
        self.efree[eng] = st_ + 0.1
        self.fin[tk] = st_ + 4.0
        self.streams[eng].append(('o', [lambda e: e.dma_start(out=out, in_=in_, **kw)], sem, 16))
        if self.cnt[sem] == 16 or sem.startswith('wl') or sem.startswith('x'):
            self.snap[tk] = dict(self.seen[eng])
        self._mark(tk, R, W)

    def fix(self, sem, bufs):
        for b in bufs:
            b.w = (sem, self.cnt[sem])

    def final_wait(self, eng, sems):
        for s in sems:
            if s in self.cnt and self.cnt[s] > 0:
                self.streams[eng].append(('w', s, self.cnt[s]))

    def emit(self):
        nc = self.nc

        def run(key, e):
            for it in self.streams[key]:
                if it[0] == 'w':
                    e.wait_ge(self.sems[it[1]], it[2])
                else:
                    _, fns, s, inc = it
                    for f in fns[:-1]:
                        f(e)
                    ins = fns[-1](e)
                    ins.then_inc(self.sems[s], inc)

        with nc.Block() as block:
            @block.tensor
            def _(e):
                run('pe', e)

            @block.scalar
            def _(e):
                run('act', e)

            @block.vector
            def _(e):
                run('dve', e)

            @block.gpsimd
            def _(e):
                run('pool', e)

            @block.sync
            def _(e):
                run('sp', e)


def _n(ap):
    n = 1
    for d in list(ap.shape)[1:]:
        n *= int(d)
    return n


def _c(f, c):
    f.c = c
    return f


def MM(out, lhsT, rhs, start, stop):
    n = _n(rhs)
    c = max(0.03, n / 2400.0) * (4.0 if rhs.dtype == F32 else 1.0) + 0.005
    return _c(lambda e: e.matmul(out=out, lhsT=lhsT, rhs=rhs, start=start, stop=stop, skip_group_check=True), c)


def TR(out, in_, ident):
    return _c(lambda e: e.transpose(out=out, in_=in_, identity=ident), 0.1)


def ACT(out, in_, func, **kw):
    return _c(lambda e: e.activation(out=out, in_=in_, func=func, **kw), (_n(out) + 230) / 1200.0)


def TT(out, in0, in1, op):
    return _c(lambda e: e.tensor_tensor(out=out, in0=in0, in1=in1, op=op), (_n(out) + 100) / 960.0)


def TS(out, in0, s1, s2, op0, op1=None):
    c = (_n(out) + 100) / 960.0
    if op1 is None:
        return _c(lambda e: e.tensor_scalar(out=out, in0=in0, scalar1=s1, scalar2=None, op0=op0), c)
    return _c(lambda e: e.tensor_scalar(out=out, in0=in0, scalar1=s1, scalar2=s2, op0=op0, op1=op1), c)


def STT(out, in0, scalar, in1, op0, op1):
    return _c(lambda e: e.scalar_tensor_tensor(out=out, in0=in0, scalar=scalar, in1=in1, op0=op0, op1=op1),
              (_n(out) + 150) / 960.0)


def CP(out, in_):
    return _c(lambda e: e.tensor_copy(out=out, in_=in_), (_n(out) / 2 + 100) / 960.0)


def RED(out, in_):
    return _c(lambda e: e.tensor_reduce(out=out, in_=in_, axis=AX.X, op=ALU.add), (_n(in_) + 100) / 960.0)


def MS(ap, val):
    return _c(lambda e: e.memset(ap, val), (_n(ap) + 60) / 960.0)


def RCP(out, in_):
    return _c(lambda e: e.reciprocal(out=out, in_=in_), (_n(out) + 100) / 960.0)


def build(nseq, S, dbg=None):
    nc = bass.Bass("TRN2", target_bir_lowering=False)
    nst_seq = S // T
    NST = nseq * nst_seq
    with ExitStack() as stack:
        C = Ctx(nc, stack)

        def dram(name, shape, dt=F32, kind="ExternalInput"):
            return nc.dram_tensor(name, list(shape), dt, kind=kind)

        x_d = dram("x", [nseq, S, D]).ap()
        y_d = dram("y", [nseq, S, D], kind="ExternalOutput").ap()
        norm1_d = dram("norm1", [2, D])
        w_in_d = dram("w_in", [2, D, DIN]).ap()
        q_norm_d = dram("q_norm", [2, 64])
        k_norm_d = dram("k_norm", [2, 64])
        sinks_d = dram("sinks", [2, 8])
        rel_bias_d = dram("rel_bias", [32, 8]).ap()
        hg_lb_d = dram("hg_lb", [2, 512])
        hg_norm_d = dram("hg_norm", [2, 128])
        w_pa_d = dram("w_pa", [2, 512, D]).ap()
        w_ph_d = dram("w_ph", [2, 512, D]).ap()
        w_out_d = dram("w_out", [2, D, D]).ap()
        norm2_d = dram("norm2", [2, D])
        w_up_d = dram("w_up", [2, D, NUP]).ap()
        conv_w_d = dram("conv_w", [2, 3, NUP]).ap()
        conv_b_d = dram("conv_b", [2, NUP]).ap()
        w_down_d = dram("w_down", [2, DFF, D]).ap()
        dbg_out = {}
        if dbg:
            for name, shape in dbg.items():
                dbg_out[name] = dram("dbg_" + name, shape, kind="ExternalOutput").ap()

        def sb(name, shape, dt=F32):
            return stack.enter_context(nc.sbuf_tensor(name, list(shape), dt))

        POOLS = {'f512': ([128, 512], F32, 8), 'b512': ([128, 512], BF16, 15), 'b1024': ([128, 1024], BF16, 4),
                 's8': ([128, 8], F32, 24), 'f128': ([128, 128], F32, 4), 'ucat': ([128, 514], F32, 3)}
        _pools = {}

        class Scope:
            def __init__(self):
                self.items = []

        cur = [Scope()]

        def rot(name):
            if name not in _pools:
                shape, dt, n = POOLS[name]
                _pools[name] = [(sb("%s_%d" % (name, i), shape, dt), Buf()) for i in range(n)]
            assert _pools[name], "scratch pool %s exhausted" % name
            item = _pools[name].pop(0)
            cur[0].items.append((name, item))
            return item

        def free(*bufs):
            for b in bufs:
                for ent in cur[0].items:
                    if ent[1][1] is b:
                        cur[0].items.remove(ent)
                        _pools[ent[0]].append(ent[1])
                        break
                else:
                    raise AssertionError("free of unowned tile")

        def flush(scope=None):
            sc = scope or cur[0]
            for name, item in sc.items:
                _pools[name].append(item)
            sc.items = []

        def gstep(gs):
            prev = cur[0]
            cur[0] = gs[1]
            try:
                next(gs[0])
                alive = True
            except StopIteration:
                flush(gs[1])
                alive = False
            cur[0] = prev
            return alive

        _pools['bank'] = [(stack.enter_context(nc.psum_tensor("ps%d" % i, [128, 512], F32)), Buf()) for i in range(8)]

        def bank():
            return rot('bank')

        def gbank():
            while not _pools['bank']:
                yield
            return rot('bank')

        A_chunks = [(0, 512), (512, 768), (768, 1280), (1280, 1792), (1792, 2304), (2304, 2816)]
        B_chunks = [(2816, 3328), (3328, 3840), (3840, 4352), (4352, 4864)]
        def tile_list(l):
            tl = []
            for (c0, c1) in A_chunks + B_chunks:
                tl.append((8, c1 - c0, [(0, w_in_d[l, :, c0:c1])]))
            for n in range(2):
                tl.append((4, 512, [(0, w_pa_d[l, :, n * 512:(n + 1) * 512])]))
                tl.append((4, 512, [(0, w_ph_d[l, :, n * 512:(n + 1) * 512])]))
            for n in range(2):
                tl.append((8, 512, [(0, w_out_d[l, :, n * 512:(n + 1) * 512])]))
            for i in range(11):
                tl.append((8, 512, [(0, w_up_d[l, :, 256 * i:256 * i + 256]),
                                    (256, w_up_d[l, :, DFF + 256 * i:DFF + 256 * i + 256])]))
            for n in range(2):
                for kg in range(3):
                    kc = 8 if kg < 2 else 6
                    tl.append((kc, 512, [(0, w_down_d[l, kg * 1024:kg * 1024 + kc * 128, n * 512:(n + 1) * 512])]))
            return tl

        tls = [tile_list(l) for l in range(2)]
        NTL = len(tls[0])
        wsc = [dram("wsc%d" % l, [NTL, 128, 4096], BF16, kind="Internal").ap() for l in range(2)]
        wsc_b = [[Buf(const=True) for _ in range(NTL)] for l in range(2)]

        def wsc_view(l, i):
            kc, w, _ = tls[l][i]
            return wsc[l][i][:, 0:kc * w].rearrange("p (k n) -> p k n", k=kc)

        NGRP = (NTL + 3) // 4
        cast_done = [0]

        def cast_next():
            gi = cast_done[0]
            if gi >= 2 * NGRP:
                return
            cast_done[0] += 1
            l, g = gi // NGRP, gi % NGRP
            sem = 'wc%d_%d' % (l, g)
            grp = []
            for i in range(g * 4, min(g * 4 + 4, NTL)):
                kc, w, parts = tls[l][i]
                v = wsc_view(l, i)
                for (d0, src) in parts:
                    wd = src.shape[1]
                    C.dma('pool', out=v[:, :, d0:d0 + wd], in_=src.rearrange("(k p) n -> p k n", p=128),
                          W=[wsc_b[l][i]], sem=sem)
                grp.append(wsc_b[l][i])
            C.fix(sem, grp)

        CAST_AHEAD = 3

        slots = [(sb("wslot%d" % i, [128, 8, 512], BF16), Buf()) for i in range(NSLOT)]
        wseq_total = NST * 2 * NTL
        wstate = {'load': 0, 'cons': 0}

        def w_load():
            n = wstate['load']
            if n >= wseq_total:
                return
            wstate['load'] += 1
            l = (n // NTL) % 2
            i = n % NTL
            if n < 2 * NTL:
                while cast_done[0] < min(2 * NGRP, l * NGRP + i // 4 + 1 + CAST_AHEAD):
                    cast_next()
            kc, w, _ = tls[l][i]
            st_, sbuf_ = slots[n % NSLOT]
            C.dma('sp', out=st_[:, 0:kc, 0:w], in_=wsc_view(l, i), R=[wsc_b[l][i]], W=[sbuf_],
                  sem='wl%d' % (n % NSLOT))

        def w_acquire(l_expect, i_expect):
            n = wstate['cons']
            assert (n // NTL) % 2 == l_expect and n % NTL == i_expect, (n, l_expect, i_expect)
            wstate['cons'] += 1
            return slots[n % NSLOT]

        xs = sb("xs", [128, NSUB, D]); xb = [Buf() for _ in range(NSUB)]
        xs1 = sb("xs1", [128, NSUB, D]); xb1 = [Buf() for _ in range(NSUB)]
        XS = [xs, xs1]; XB = [xb, xb1]
        hT = sb("hT", [128, 8, T], BF16); hTb = [Buf() for _ in range(NSUB)]
        sigT = sb("sigT", [128, 16, T], BF16); sigTb = [Buf() for _ in range(16)]
        aT = sb("aT", [128, 4, T], BF16); aTb = [Buf() for _ in range(NSUB)]
        rT = sb("rT", [128, 4, T], BF16); rTb = [Buf() for _ in range(NSUB)]
        qn = [sb("qn%d" % j, [128, 512], BF16) for j in range(NSUB)]; qnb = [Buf() for _ in range(NSUB)]
        kn = [sb("kn%d" % j, [128, 128], BF16) for j in range(NSUB)]; knb = [Buf() for _ in range(NSUB)]
        sq = [sb("sq%d" % j, [128, 512], BF16) for j in range(NSUB)]; sqb = [Buf() for _ in range(NSUB)]
        kk = [sb("kk%d" % j, [128, 512]) for j in range(NSUB)]; kkb = [Buf() for _ in range(NSUB)]
        kkh = [sb("kkh%d" % j, [128, 512], BF16) for j in range(NSUB)]; kkhb = [Buf() for _ in range(NSUB)]
        iv = [sb("iv%d" % j, [128, 512], BF16) for j in range(NSUB)]; ivb = [Buf() for _ in range(NSUB)]
        gsil = [sb("gsil%d" % j, [128, 512], BF16) for j in range(NSUB)]; gsilb = [Buf() for _ in range(NSUB)]
        actT_c = [sigT[:, i, :] for i in range(16)] + [sq[j][:, :] for j in range(NSUB)] + [iv[0][:, :], iv[1][:, :]]
        actTb = sigTb + sqb + [ivb[0], ivb[1]]
        KT = [[sb("KT%d_%d" % (l, j), [128, 256], BF16) for j in range(5)] for l in range(2)]
        KTb = [[Buf() for _ in range(5)] for l in range(2)]
        for l in range(2):
            for j in range(5):
                C.op('pool', MS(KT[l][j][:, :], 0.0), W=[KTb[l][j]])
        V = [[sb("V%d_%d" % (l, j), [128, 2, 65], BF16) for j in range(5)] for l in range(2)]
        Vb = [[Buf() for _ in range(5)] for l in range(2)]
        for l in range(2):
            for j in range(5):
                C.op('pool', MS(V[l][j][:, :, 64:65], 1.0), W=[Vb[l][j]])
        Sst = [sb("S%d" % l, [128, 4, 128]) for l in range(2)]; Sstb = [Buf() for _ in range(2)]
        Sbf = [[sb("Sbf%d_%d" % (l, i), [128, 4, 128], BF16) for i in range(2)] for l in range(2)]
        Sbfb = [[Buf() for _ in range(2)] for _ in range(2)]
        ccar = [sb("ccar%d" % l, [128, 44, 2]) for l in range(2)]
        ccarb = [[Buf() for _ in range(44)] for l in range(2)]

        def load_x(st):
            seq, pos = st // nst_seq, (st % nst_seq) * T
            xi = st % 2
            for j in range(NSUB):
                C.dma('sp', out=XS[xi][:, j, :], in_=x_d[seq, pos + j * 128:pos + (j + 1) * 128, :], W=[XB[xi][j]],
                      sem='xl%d_%d' % (xi, j))

        load_x(0)

        def cbuf():
            return Buf(const=True)

        ones_f = sb("ones_f", [128, 128]); ones_b = cbuf()
        identf = sb("identf", [128, 128]); identf_b = cbuf()
        ident = sb("ident", [128, 128], BF16); ident_b = cbuf()
        U = sb("U", [128, 128]); U_b = cbuf()
        maskU4 = sb("maskU4", [128, 4, 128], BF16); maskU4_b = cbuf()
        chunkind = sb("chunkind", [128, 2]); chunkind_b = cbuf()
        neghalf = sb("neghalf", [128, 8]); neghalf_b = cbuf()

        C.op('pool', MS(ones_f[:, :], 1.0), W=[ones_b])
        C.op('pool', lambda e: e.affine_select(out=identf[:, :], in_=ones_f[:, :], pattern=[[-1, 128]],
                                                compare_op=ALU.is_equal, fill=0.0, base=0, channel_multiplier=1),
             R=[ones_b], W=[identf_b])
        C.op('dve', CP(ident[:, :], identf[:, :]), R=[identf_b], W=[ident_b])
        antif = sb("antif", [128, 128]); antif_b = Buf()
        anti = sb("anti", [128, 128], BF16); anti_b = cbuf()
        C.op('pool', lambda e: e.affine_select(out=antif[:, :], in_=ones_f[:, :], pattern=[[1, 128]],
                                                compare_op=ALU.is_equal, fill=0.0, base=-127, channel_multiplier=1),
             R=[ones_b], W=[antif_b])
        C.op('dve', CP(anti[:, :], antif[:, :]), R=[antif_b], W=[anti_b])
        C.op('pool', lambda e: e.affine_select(out=U[:, :], in_=ones_f[:, :], pattern=[[1, 128]],
                                                compare_op=ALU.is_ge, fill=0.0, base=0, channel_multiplier=-1),
             R=[ones_b], W=[U_b])
        C.op('pool', MS(U[0:64, 64:128], 0.0), W=[U_b])
        for h in range(4):
            C.op('dve', CP(maskU4[:, h, :], U[:, :]), R=[U_b], W=[maskU4_b])
        C.op('pool', MS(chunkind[:, :], 0.0), W=[chunkind_b])
        C.op('pool', MS(chunkind[0:64, 0:1], 1.0), W=[chunkind_b])
        C.op('pool', MS(chunkind[64:128, 1:2], 1.0), W=[chunkind_b])
        C.op('pool', MS(neghalf[:, :], -0.5), W=[neghalf_b])

        def bc_src(th, off, ap):
            return bass.AP(th, off, ap)

        hgn_bc, gqk_bc, oml_bc, esink, g1T, g2T, cw = [], [], [], [], [], [], []
        hgn_b, gqk_b, oml_b, esink_b, g1T_b, g2T_b, cw_b = [], [], [], [], [], [], []
        lb_bc = xs1[:, 0, :].rearrange("p (a n) -> p a n", a=2); lb_bc_b = xb1[0]
        C.dma('sp', out=lb_bc, in_=bc_src(hg_lb_d, 0, [[0, 128], [512, 2], [1, 512]]), W=[lb_bc_b], sem='cst1')
        for l in range(2):
            t_ = sb("hgn_bc%d" % l, [128, 128]); b_ = cbuf()
            C.dma('sp', out=t_[:, :], in_=bc_src(hg_norm_d, l * 128, [[0, 128], [1, 128]]), W=[b_], sem='cst2')
            hgn_bc.append(t_); hgn_b.append(b_)
            gq = sb("gq_bc%d" % l, [128, 2, 64]); gqb = Buf()
            gk = sb("gk_bc%d" % l, [128, 2, 64]); gkb = Buf()
            C.dma('sp', out=gq[:, :, :], in_=bc_src(q_norm_d, l * 64, [[0, 128], [0, 2], [1, 64]]), W=[gqb], sem='cst3')
            C.dma('sp', out=gk[:, :, :], in_=bc_src(k_norm_d, l * 64, [[0, 128], [0, 2], [1, 64]]), W=[gkb], sem='cst4')
            t_ = sb("gqk_bc%d" % l, [128, 2, 64]); b_ = cbuf()
            C.op('dve', TT(t_[:, :, :], gq[:, :, :], gk[:, :, :], ALU.mult), R=[gqb, gkb], W=[b_])
            gqk_bc.append(t_); gqk_b.append(b_)
            if l == 0:
                t_ = None; b_ = None
            else:
                t_ = sb("oml_bc%d" % l, [128, 512]); b_ = cbuf()
                dl = xs1[:, 1, 0:512]; dlb = xb1[1]
                C.op('dve', TT(dl, lb_bc[:, 0, :], lb_bc[:, 1, :], ALU.subtract), R=[lb_bc_b], W=[dlb])
                C.op('act', ACT(t_[:, :], dl, AF.Sigmoid), R=[dlb], W=[b_])
            oml_bc.append(t_); oml_b.append(b_)
            sk = sb("sink_bc%d" % l, [128, 8]); skb = Buf()
            C.dma('sp', out=sk[:, :], in_=bc_src(sinks_d, l * 8, [[0, 128], [1, 8]]), W=[skb], sem='cst5')
            t_ = sb("esink%d" % l, [128, 8]); b_ = cbuf()
            C.op('act', ACT(t_[:, :], sk[:, :], AF.Exp), R=[skb], W=[b_])
            esink.append(t_); esink_b.append(b_)
            for (lst, lstb, src, nm) in ((g1T, g1T_b, norm1_d, "g1T"), (g2T, g2T_b, norm2_d, "g2T")):
                t_ = sb("%s%d" % (nm, l), [128, 8]); b_ = cbuf()
                C.dma('sp', out=t_[:, :], in_=bc_src(src, l * D, [[1, 128], [128, 8]]), W=[b_], sem='cst6',
                      allow_slow_non_contiguous=True)
                lst.append(t_); lstb.append(b_)
            stg = xs1[0:44, 2 + l, 0:512].rearrange("p (a n) -> p a n", a=4); stgb = xb1[2 + l]
            for j in range(3):
                C.dma('sp', out=stg[:, j, :], in_=conv_w_d[l, j, :].rearrange("(c p) -> c p", p=128), W=[stgb], sem='cst7')
            C.dma('sp', out=stg[:, 3, :], in_=conv_b_d[l, :].rearrange("(c p) -> c p", p=128), W=[stgb], sem='cst8')
            t_ = sb("cw%d" % l, [128, 4, 44]); b_ = cbuf()
            ps, pb = bank()
            C.op('pe', [TR(ps[:, j * 44:(j + 1) * 44], stg[:, j, :], identf[0:44, 0:44]) for j in range(4)],
                 R=[stgb, identf_b], W=[pb])
            C.op('dve', CP(t_[:, :, :], ps[:, 0:176].rearrange("p (j c) -> p j c", j=4)), R=[pb], W=[b_])
            flush()
            cw.append(t_); cw_b.append(b_)

        bk = t5_ranges()
        rbT = sb("rbT", [8, 32]); rbT_b = Buf()
        C.dma('sp', out=rbT[:, :], in_=rel_bias_d.rearrange("b h -> h b"), W=[rbT_b], sem='cst9',
              allow_slow_non_contiguous=True)
        biasT = {}
        biasT_b = cbuf()

        def build_bias():
            e_sb = sb("e_sb", [8, 384]); e_sbb = Buf()
            C.op('dve', MS(e_sb[:, :], NEG), W=[e_sbb])
            C.op('dve', CP(e_sb[:, 127:143], rbT[:, 0:16]), R=[rbT_b], W=[e_sbb])
            for b in range(16, 32):
                idx = np.nonzero(bk == b)[0]
                if len(idx) == 0:
                    continue
                lo, hi = int(idx[0]), int(idx[-1]) + 1
                assert np.all(bk[lo:hi] == b)
                C.op('dve', CP(e_sb[:, 127 + lo:127 + hi], rbT[:, b:b + 1].to_broadcast([8, hi - lo])), R=[rbT_b], W=[e_sbb])
            e_d = dram("e_scr", [8, 384], kind="Internal")
            e_db = Buf()
            C.dma('sp', out=e_d.ap(), in_=e_sb[:, :], R=[e_sbb], W=[e_db], sem='cst10')
            for hk in range(2):
                for kind, off in (('prev', 255), ('cur', 127)):
                    t_ = sb("biasT_%d%s" % (hk, kind), [128, 4, 128], BF16)
                    stg_, stgfb = rot('f512')
                    stgf = stg_[:, :].rearrange("p (g q) -> p g q", g=4)
                    for g in range(4):
                        h = hk * 4 + g
                        C.dma('sp', out=stgf[:, g, :], in_=bc_src(e_d, h * 384 + off - 127, [[1, 128], [1, 128]]),
                              R=[e_db], W=[stgfb], sem='cst11')
                    C.op('dve', CP(t_[:, :, :], stgf), R=[stgfb], W=[biasT_b])
                    biasT[(hk, kind)] = t_
            flush()

        def dump(name, ap, bufs):
            if name in dbg_out:
                C.dma('sp', out=dbg_out[name], in_=ap, R=bufs, sem='dbg')

        def windowed(pending, W, always=(), always_every=1, always_start=0):
            pending = [[g, Scope(), 0.0] for g in pending]
            active = []
            for g in always:
                active.append([g, Scope(), 0.0, True])
            nalways = len(active)
            while pending or active:
                while pending and sum(1 for a in active if len(a) == 3) < W:
                    ent = pending.pop(0)
                    ent[2] = min([a[2] for a in active] + [C.efree['pe']])
                    active.append(ent)
                ent = min(active, key=lambda a: a[2])
                C.step_end = 0.0
                n0 = C.nops
                alive = gstep(ent)
                if not alive:
                    active.remove(ent)
                elif C.nops == n0:
                    others = sorted(a[2] for a in active if a is not ent)
                    ent[2] = (others[0] if others else ent[2]) + 1e-3
                    if others and all(abs(o - others[0]) < 1e-2 for o in others) and len(others) > 0:
                        ent[2] = others[-1] + 1e-3
                else:
                    ent[2] = C.step_end

        def interleave(gens):
            windowed(list(gens), 99)

        flags = {}

        xflags = {}

        def norm_T(gT, gTb, xi=0):
            interleave([norm_T_j(gT, gTb, j, None, xi) for j in range(NSUB)])

        def norm_T_j(gT, gTb, j, key=None, xi=0):
            xs, xb = XS[xi], XB[xi]
            while key is not None and not xflags.get((key, j)):
                yield
            if True:
                junk, junkb = rot('b1024')
                ssq, ssqb = rot('s8')
                C.op('act', ACT(junk[:, :], xs[:, j, :], AF.Square, accum_out=ssq[:, 0:1]), R=[xb[j]], W=[junkb, ssqb])
                free(junkb)
                yield
                ms, msb = rot('s8')
                C.op('pool', TS(ms[:, 0:1], ssq[:, 0:1], 1.0 / D, EPS, ALU.mult, ALU.add), R=[ssqb], W=[msb])
                rstd, rstdb = rot('s8')
                C.op('pool', TT(rstd[:, 0:1], ms[:, 0:1], neghalf[:, 0:1], ALU.pow), R=[msb, neghalf_b], W=[rstdb])
                yield
                hb, hbb = rot('b1024')
                C.op('act', ACT(hb[:, :], xs[:, j, :], AF.Copy, scale=rstd[:, 0:1]), R=[xb[j], rstdb], W=[hbb])
                yield
                ps, pb = yield from gbank()
                psv = ps[:, :].bitcast(BF16)
                C.op('pe', [TR(psv[:, k * 128:(k + 1) * 128], hb[:, k * 128:(k + 1) * 128], ident[:, :]) for k in range(8)],
                     R=[hbb, ident_b], W=[pb])
                free(hbb)
                yield
                for half in range(2):
                    C.op('dve', TT(hT[:, 4 * half:4 * half + 4, j * 128:(j + 1) * 128],
                                   psv[:, half * 512:(half + 1) * 512].rearrange("p (k t) -> p k t", k=4),
                                   gT[:, 4 * half:4 * half + 4].unsqueeze(2).to_broadcast([128, 4, 128]), ALU.mult),
                         R=[pb, gTb], W=[hTb[j]])

        def rsq_small(src, n, scale, epsv, name):
            ms, msb = rot('s8')
            C.op('dve', TS(ms[:, 0:n], src[0][:, 0:n], scale, epsv, ALU.mult, ALU.add), R=[src[1]], W=[msb])
            r, rb = rot('s8')
            C.op('pool', TT(r[:, 0:n], ms[:, 0:n], neghalf[:, 0:n], ALU.pow), R=[msb, neghalf_b], W=[rb])
            return r, rb

        def evac_q(l, j, ps, pb):
            qf, qfb = rot('f512')
            C.op('act', ACT(qf[:, :], ps[:, :], AF.Copy), R=[pb], W=[qfb])
            sqt, sqtb = rot('f512')
            C.op('dve', TT(sqt[:, :], qf[:, :], qf[:, :], ALU.mult), R=[qfb], W=[sqtb])
            ssq8, ssq8b = rot('s8')
            C.op('dve', RED(ssq8[:, :], sqt[:, :].rearrange("p (h d) -> p h d", h=8)), R=[sqtb], W=[ssq8b])
            rq, rqb = rsq_small((ssq8, ssq8b), 8, 1.0, 64 * EPS, 'rq')
            C.op('dve', TT(qn[j][:, :].rearrange("p (q g d) -> p q g d", q=4, g=2),
                           qf[:, :].rearrange("p (g q d) -> p q g d", g=2, q=4),
                           rq[:, 0:8].rearrange("p (g q) -> p q g", g=2).unsqueeze(3).to_broadcast([128, 4, 2, 64]), ALU.mult),
                 R=[qfb, rqb], W=[qnb[j]])

        def evac_kv(l, j, ps, pb):
            kf, kfb = rot('f128')
            C.op('act', ACT(kf[:, :], ps[:, 0:128], AF.Copy), R=[pb], W=[kfb])
            C.op('act', ACT(V[l][j][:, :, 0:64], ps[:, 128:256].rearrange("p (h d) -> p h d", h=2), AF.Copy),
                 R=[pb], W=[Vb[l][j]])
            sqt, sqtb = rot('f128')
            C.op('dve', TT(sqt[:, :], kf[:, :], kf[:, :], ALU.mult), R=[kfb], W=[sqtb])
            ssq2, ssq2b = rot('s8')
            C.op('dve', RED(ssq2[:, 0:2], sqt[:, :].rearrange("p (h d) -> p h d", h=2)), R=[sqtb], W=[ssq2b])
            rk, rkb = rsq_small((ssq2, ssq2b), 2, 1.0 / 64, EPS, 'rk')
            kt, ktb = rot('f128')
            C.op('dve', TT(kt[:, :].rearrange("p (h d) -> p h d", h=2), kf[:, :].rearrange("p (h d) -> p h d", h=2),
                           rk[:, 0:2].unsqueeze(2).to_broadcast([128, 2, 64]), ALU.mult), R=[kfb, rkb], W=[ktb])
            C.op('pool', TT(kn[j][:, :].rearrange("p (h d) -> p h d", h=2), kt[:, :].rearrange("p (h d) -> p h d", h=2),
                            gqk_bc[l][:, :, :], ALU.mult), R=[ktb, gqk_b[l]], W=[knb[j]])

        def evac_qr(l, j, ps, pb):
            C.op('act', ACT(sq[j][:, :], ps[:, :], AF.Silu), R=[pb], W=[sqb[j]])

        def evac_f(l, j, ps, pb):
            if oml_bc[l] is None:
                C.op('act', ACT(kk[j][:, :], ps[:, :], AF.Sigmoid, scale=-1.0), R=[pb], W=[kkb[j]])
            else:
                sn, snb = rot('f512')
                C.op('act', ACT(sn[:, :], ps[:, :], AF.Sigmoid, scale=-1.0), R=[pb], W=[snb])
                C.op('pool', TT(kk[j][:, :], sn[:, :], oml_bc[l][:, :], ALU.mult), R=[snb, oml_b[l]], W=[kkb[j]])
            C.op('dve', CP(kkh[j][:, :], kk[j][:, :]), R=[kkb[j]], W=[kkhb[j]])

        def logf_batch():
            for j in range(NSUB):
                C.op('act', ACT(kk[j][:, :], kk[j][:, :], AF.Ln, scale=-1.0, bias=1.0), R=[kkb[j]], W=[kkb[j]])

        def evac_i(l, j, ps, pb):
            C.op('dve', CP(iv[j][:, :], ps[:, :]), R=[pb], W=[ivb[j]])

        def evac_g(l, j, ps, pb):
            gs, gsb = rot('f512')
            C.op('act', ACT(gs[:, :], ps[:, :], AF.Silu), R=[pb], W=[gsb])
            C.op('pool', TT(gsil[j][:, :].rearrange("p (h d) -> p h d", h=4), gs[:, :].rearrange("p (h d) -> p h d", h=4),
                            hgn_bc[l][:, :].unsqueeze(1).to_broadcast([128, 4, 128]), ALU.mult),
                 R=[gsb, hgn_b[l]], W=[gsilb[j]])

        evacA = [evac_q, evac_kv, evac_qr, evac_f, evac_i, evac_g]

        def proj(l):
            for c, (c0, c1) in enumerate(A_chunks):
                wt, wb = w_acquire(l, c)
                w = c1 - c0
                for j in range(NSUB):
                    ps, pb = bank()
                    C.op('pe', [MM(ps[:, 0:w], hT[:, k, j * 128:(j + 1) * 128], wt[:, k, 0:w], k == 0, k == 7)
                                for k in range(8)], R=[hTb[j], wb], W=[pb])
                    evacA[c](l, j, ps, pb)
                    flush()
                if c == 3:
                    logf_batch()
                w_load()

        def gates(l):
            for t in range(4):
                wt, wb = w_acquire(l, 6 + t)
                for ii in range(4):
                    i = t * 4 + ii
                    ps, pb = yield from gbank()
                    C.op('pe', [MM(ps[:, :], wt[:, k, ii * 128:(ii + 1) * 128], hT[:, k, :], k == 0, k == 7)
                                for k in range(8)], R=hTb + [wb], W=[pb])
                    C.op('act', ACT(sigT[:, i, :], ps[:, :], AF.Tanh, scale=0.5), R=[pb], W=[sigTb[i]])
                    free(pb)
                    if ii == 3:
                        w_load()
                    yield

        def attention(l, j, has_prev):
            pj = 4 if j == 0 else j - 1
            ps, pb = yield from gbank()
            psv = ps[:, :].bitcast(BF16)
            C.op('pe', [TR(psv[:, p * 128:(p + 1) * 128], qn[j][:, p * 128:(p + 1) * 128], ident[:, :]) for p in range(4)],
                 R=[qnb[j], ident_b], W=[pb])
            QT, QTb = rot('b512')
            C.op('act', ACT(QT[:, :], psv[:, 0:512], AF.Copy), R=[pb], W=[QTb])
            free(pb)
            ps2, pb2 = yield from gbank()
            psv2 = ps2[:, :].bitcast(BF16)
            C.op('pe', [TR(psv2[:, 0:128], kn[j][:, :], ident[:, :])], R=[knb[j], ident_b], W=[pb2])
            C.op('dve', CP(KT[l][j][0:64, 0:128], psv2[0:64, 0:128]), R=[pb2], W=[KTb[l][j]])
            C.op('dve', CP(KT[l][j][64:128, 128:256], psv2[64:128, 0:128]), R=[pb2], W=[KTb[l][j]])
            free(pb2)
            flags[('kt', j)] = True
            yield
            while j > 0 and not flags.get(('kt', j - 1)):
                yield
            kinds = ([('prev', pj)] if has_prev else []) + [('cur', j)]
            PT = {}
            for hk in range(2):
                for kind, jj in kinds:
                    ps, pb = yield from gbank()
                    C.op('pe', [MM(ps[:, :], KT[l][jj][:, hk * 128:(hk + 1) * 128], QT[:, :], True, False),
                                MM(ps[:, :], anti[:, :], biasT[(hk, kind)][:, :, :].rearrange("p g q -> p (g q)"), False, True)],
                         R=[KTb[l][jj], QTb, anti_b, biasT_b], W=[pb])
                    pt, ptb = rot('b512')
                    C.op('act', ACT(pt[:, :], ps[:, :], AF.Exp), R=[pb], W=[ptb])
                    free(pb)
                    PT[(hk, kind)] = (pt, ptb)
                yield
            free(QTb)
            aout, aoutb = rot('b512')
            yield
            for half in range(2):
                hk = half
                ps, pb = yield from gbank()
                fns = []
                Rl = []
                for g in range(4):
                    for idx, (kind, jj) in enumerate(kinds):
                        pt, ptb = PT[(hk, kind)]
                        fns.append(MM(ps[:, g * 65:(g + 1) * 65], pt[:, g * 128:(g + 1) * 128], V[l][jj][:, hk, :],
                                      idx == 0, idx == len(kinds) - 1))
                        Rl += [ptb, Vb[l][jj]]
                C.op('pe', fns, R=Rl, W=[pb])
                free(*[PT[(hk, kind)][1] for kind, _ in kinds])
                psv3 = ps[:, 0:260].rearrange("p (g d) -> p g d", g=4)
                den, denb = rot('s8')
                C.op('dve', TT(den[:, 0:4].unsqueeze(2), psv3[:, :, 64:65], esink[l][:, half * 4:half * 4 + 4].unsqueeze(2), ALU.add),
                     R=[pb, esink_b[l]], W=[denb])
                rden, rdenb = rot('s8')
                C.op('dve', RCP(rden[:, 0:4], den[:, 0:4]), R=[denb], W=[rdenb])
                C.op('dve', TT(aout[:, half * 256:(half + 1) * 256].rearrange("p (g d) -> p g d", g=4), psv3[:, :, 0:64],
                               rden[:, 0:4].unsqueeze(2).to_broadcast([128, 4, 64]), ALU.mult), R=[pb, rdenb], W=[aoutb])
                free(denb, rdenb)
                free(pb)
                yield
            ps, pb = yield from gbank()
            psv = ps[:, :].bitcast(BF16)
            C.op('pe', [TR(psv[:, c * 128:(c + 1) * 128], aout[:, c * 128:(c + 1) * 128], ident[:, :]) for c in range(4)],
                 R=[aoutb, ident_b], W=[pb])
            free(aoutb)
            C.op('act', ACT(aT[:, :, j * 128:(j + 1) * 128], psv[:, 0:512].rearrange("p (c t) -> p c t", c=4), AF.Copy),
                 R=[pb], W=[aTb[j]])
            if j == NSUB - 1:
                C.op('pool', CP(KT[l][4][:, :], KT[l][3][:, :]), R=[KTb[l][3]], W=[KTb[l][4]])
                C.op('pool', CP(V[l][4][:, :, 0:64], V[l][3][:, :, 0:64]), R=[Vb[l][3]], W=[Vb[l][4]])
            if l == 0 and j == 1:
                dump('aout', aout[:, :], [aoutb])

        def hgrn(l, j):
            lf, lfb = kk[j], kkb[j]
            ps_b, pb_b = yield from gbank()
            C.op('pe', [MM(ps_b[:, :], U[:, :], lf[:, :], True, True)], R=[U_b, lfb], W=[pb_b])
            ps_l, pb_l = yield from gbank()
            C.op('pe', [MM(ps_l[:, h * 2:(h + 1) * 2], lf[:, h * 128:(h + 1) * 128], chunkind[:, :], True, True) for h in range(4)],
                 R=[lfb, chunkind_b], W=[pb_l])
            ebl, eblb = rot('s8')
            C.op('act', ACT(ebl[:, :], ps_l[:, 0:8], AF.Exp), R=[pb_l], W=[eblb])
            free(pb_l)
            yield
            eb, ebb = rot('f512')
            C.op('act', ACT(eb[:, :], ps_b[:, :], AF.Exp), R=[pb_b], W=[ebb])
            enb, enbb = rot('f512')
            C.op('act', ACT(enb[:, :], ps_b[:, :], AF.Exp, scale=-1.0), R=[pb_b], W=[enbb])
            free(pb_b)
            qp, qpb = rot('b512')
            C.op('dve', TT(qp[:, :], sq[j][:, :], eb[:, :], ALU.mult), R=[sqb[j], ebb], W=[qpb])
            free(ebb)
            kp, kpb = rot('b512')
            C.op('dve', TT(kp[:, :], kkh[j][:, :], enb[:, :], ALU.mult), R=[kkhb[j], enbb], W=[kpb])
            free(enbb)
            yield
            ps_t, pb_t = yield from gbank()
            psv = ps_t[:, :].bitcast(BF16)
            C.op('pe', [TR(psv[:, h * 128:(h + 1) * 128], qp[:, h * 128:(h + 1) * 128], ident[:, :]) for h in range(4)] +
                 [TR(psv[:, 512 + h * 128:512 + (h + 1) * 128], kp[:, h * 128:(h + 1) * 128], ident[:, :]) for h in range(4)],
                 R=[qpb, kpb, ident_b], W=[pb_t])
            free(qpb)
            qkT, qkTb = rot('b1024')
            C.op('act', ACT(qkT[:, :], psv[:, :], AF.Copy), R=[pb_t], W=[qkTb])
            free(pb_t)
            yield
            ps_a, pb_a = yield from gbank()
            C.op('pe', [MM(ps_a[:, h * 128:(h + 1) * 128], qkT[:, 512 + h * 128:512 + (h + 1) * 128],
                           qkT[:, h * 128:(h + 1) * 128], True, True) for h in range(4)], R=[qkTb], W=[pb_a])
            attm, attmb = rot('b512')
            C.op('dve', TT(attm[:, :], ps_a[:, :], maskU4[:, :, :].rearrange("p h t -> p (h t)"), ALU.mult),
                 R=[pb_a, maskU4_b], W=[attmb])
            free(pb_a)
            yield

            def hv(ap):
                return ap.rearrange("p (h v) -> p h v", h=4)

            ps_m0, pb_m0 = yield from gbank()
            C.op('pe', [MM(ps_m0[:, h * 128:(h + 1) * 128], kp[0:64, h * 128:(h + 1) * 128],
                           iv[j][0:64, h * 128:(h + 1) * 128], True, True) for h in range(4)],
                 R=[kpb, ivb[j]], W=[pb_m0])
            ps_m1, pb_m1 = yield from gbank()
            C.op('pe', [MM(ps_m1[:, h * 128:(h + 1) * 128], kp[64:128, h * 128:(h + 1) * 128],
                           iv[j][64:128, h * 128:(h + 1) * 128], True, True) for h in range(4)],
                 R=[kpb, ivb[j]], W=[pb_m1])
            free(kpb)
            eblv = ebl[:, :].rearrange("p (h c) -> p h c", c=2)
            e01, e01b = rot('s8')
            C.op('dve', TT(e01[:, 0:4].unsqueeze(2), eblv[:, :, 0:1], eblv[:, :, 1:2], ALU.mult), R=[eblb], W=[e01b])
            yield
            c0, c0b = rot('f512')
            C.op('dve', TT(hv(c0[:, :]), hv(ps_m0[:, :]), e01[:, 0:4].unsqueeze(2).to_broadcast([128, 4, 128]), ALU.mult),
                 R=[pb_m0, e01b], W=[c0b])
            c1, c1b = rot('f512')
            C.op('dve', TT(hv(c1[:, :]), hv(ps_m1[:, :]), eblv[:, :, 1:2].to_broadcast([128, 4, 128]), ALU.mult),
                 R=[pb_m1, eblb], W=[c1b])
            free(pb_m1)
            C.op('pool', TT(c0[:, :], c0[:, :], c1[:, :], ALU.add), R=[c0b, c1b], W=[c0b])
            free(c1b)
            yield
            while j > 0 and not flags.get(('st', j - 1)):
                yield
            Sflat = Sst[l][:, :, :].rearrange("p h v -> p (h v)")
            stmp, stmpb = rot('f512')
            C.op('dve', TT(stmp[:, :], ps_m0[:, :], Sflat, ALU.add), R=[pb_m0, Sstb[l]], W=[stmpb])
            free(pb_m0)
            S1bf_, S1bfb = rot('b512')
            S1bf = hv(S1bf_[:, :])
            C.op('dve', TT(S1bf, hv(stmp[:, :]), eblv[:, :, 0:1].to_broadcast([128, 4, 128]), ALU.mult),
                 R=[stmpb, eblb], W=[S1bfb])
            free(stmpb)
            C.op('dve', TT(Sst[l][:, :, :], Sst[l][:, :, :], e01[:, 0:4].unsqueeze(2).to_broadcast([128, 4, 128]), ALU.mult),
                 R=[Sstb[l], e01b], W=[Sstb[l]])
            C.op('dve', TT(Sflat, Sflat, c0[:, :], ALU.add), R=[Sstb[l], c0b], W=[Sstb[l]])
            free(c0b, e01b, eblb)
            while j > 0 and not flags.get(('o', j - 1)):
                yield
            C.op('act', ACT(Sbf[l][(j + 1) % 2][:, :, :], Sst[l][:, :, :], AF.Copy), R=[Sstb[l]], W=[Sbfb[l][(j + 1) % 2]])
            flags[('st', j)] = True
            yield
            ps_o, pb_o = yield from gbank()
            fns = []
            for h in range(4):
                fns.append(MM(ps_o[:, h * 128:(h + 1) * 128], attm[:, h * 128:(h + 1) * 128], iv[j][:, h * 128:(h + 1) * 128], True, False))
                fns.append(MM(ps_o[0:64, h * 128:(h + 1) * 128], qkT[:, h * 128:h * 128 + 64], Sbf[l][j % 2][:, h, :], False, False))
                fns.append(MM(ps_o[64:128, h * 128:(h + 1) * 128], qkT[:, h * 128 + 64:h * 128 + 128], S1bf[:, h, :], False, True))
            C.op('pe', fns, R=[attmb, ivb[j], qkTb, Sbfb[l][j % 2], S1bfb], W=[pb_o])
            free(attmb, qkTb, S1bfb)
            flags[('o', j)] = True
            yield
            osb, osbb = rot('f512')
            C.op('act', ACT(osb[:, :], ps_o[:, :], AF.Copy), R=[pb_o], W=[osbb])
            free(pb_o)
            og, ogb = rot('f512')
            C.op('pool', TT(og[:, :], osb[:, :], gsil[j][:, :], ALU.mult), R=[osbb, gsilb[j]], W=[ogb])
            sqt, sqtb = rot('f512')
            C.op('dve', TT(sqt[:, :], osb[:, :], osb[:, :], ALU.mult), R=[osbb], W=[sqtb])
            free(osbb)
            ssq4, ssq4b = rot('s8')
            C.op('dve', RED(ssq4[:, 0:4], sqt[:, :].rearrange("p (h d) -> p h d", h=4)), R=[sqtb], W=[ssq4b])
            free(sqtb)
            rs4, rs4b = rsq_small((ssq4, ssq4b), 4, 1.0 / 128, EPS, 'rs4')
            rout, routb = rot('b512')
            C.op('dve', TT(rout[:, :].rearrange("p (h d) -> p h d", h=4), og[:, :].rearrange("p (h d) -> p h d", h=4),
                           rs4[:, 0:4].unsqueeze(2).to_broadcast([128, 4, 128]), ALU.mult), R=[ogb, rs4b], W=[routb])
            free(ogb)
            yield
            ps, pb = yield from gbank()
            psv = ps[:, :].bitcast(BF16)
            C.op('pe', [TR(psv[:, c * 128:(c + 1) * 128], rout[:, c * 128:(c + 1) * 128], ident[:, :]) for c in range(4)],
                 R=[routb, ident_b], W=[pb])
            C.op('act', ACT(rT[:, :, j * 128:(j + 1) * 128], psv[:, 0:512].rearrange("p (c t) -> p c t", c=4), AF.Copy),
                 R=[pb], W=[rTb[j]])
            if l == 0 and j == 1:
                dump('rout', rout[:, :], [routb])

        def merge(l):
            for t in range(2):
                wa, wab = w_acquire(l, 10 + 2 * t)
                wh, whb = w_acquire(l, 11 + 2 * t)
                for ii in range(4):
                    i = t * 4 + ii
                    psa, pba = bank()
                    C.op('pe', [MM(psa[:, :], wa[:, k, ii * 128:(ii + 1) * 128], aT[:, k, :], k == 0, k == 3) for k in range(4)],
                         R=aTb + [wab], W=[pba])
                    psh, pbh = bank()
                    C.op('pe', [MM(psh[:, :], wh[:, k, ii * 128:(ii + 1) * 128], rT[:, k, :], k == 0, k == 3) for k in range(4)],
                         R=rTb + [whb], W=[pbh])
                    t1, t1b = rot('f512')
                    C.op('dve', STT(t1[:, :], sigT[:, i, :], 1.0, psa[:, :], ALU.add, ALU.mult), R=[pba, sigTb[i]], W=[t1b])
                    t2, t2b = rot('f512')
                    C.op('dve', STT(t2[:, :], sigT[:, 8 + i, :], 1.0, psh[:, :], ALU.add, ALU.mult), R=[pbh, sigTb[8 + i]], W=[t2b])
                    C.op('pool', TT(hT[:, i, :], t1[:, :], t2[:, :], ALU.add), R=[t1b, t2b], W=hTb)
                    flush()
                w_load()
                w_load()

        def out_proj_g(l, key, xi):
            xs, xb = XS[xi], XB[xi]
            tiles = [w_acquire(l, 14 + n) for n in range(2)]
            for j in range(NSUB):
                for n in range(2):
                    wt, wb = tiles[n]
                    ps, pb = yield from gbank()
                    C.op('pe', [MM(ps[:, :], hT[:, k, j * 128:(j + 1) * 128], wt[:, k, :], k == 0, k == 7) for k in range(8)],
                         R=[hTb[j], wb], W=[pb])
                    xv = xs[:, j, n * 512:(n + 1) * 512]
                    C.op('dve', STT(xv, ps[:, :], 0.5, xv, ALU.mult, ALU.add), R=[pb, xb[j]], W=[xb[j]])
                    free(pb)
                xflags[(key, j)] = True
                yield
            w_load()
            w_load()

        def ffn_up(l, first):
            for t in range(11):
                wt, wb = w_acquire(l, 16 + t)
                for cc in range(2):
                    c = 2 * t + cc
                    ys = []
                    for (col0, cidx) in ((cc * 128, c), (256 + cc * 128, 22 + c)):
                        ps, pb = bank()
                        C.op('pe', [MM(ps[:, :], wt[:, k, col0:col0 + 128], hT[:, k, :], k == 0, k == 7) for k in range(8)],
                             R=hTb + [wb], W=[pb])
                        uc, ucb = rot('ucat')
                        C.op('act', ACT(uc[:, 2:514], ps[:, :], AF.Copy), R=[pb], W=[ucb])
                        y, yb = rot('f512')
                        C.op('act', ACT(y[:, :], ps[:, :], AF.Identity, scale=cw[l][:, 2, cidx:cidx + 1],
                                        bias=cw[l][:, 3, cidx:cidx + 1]), R=[pb, cw_b[l]], W=[yb])
                        if first:
                            C.op('pool', MS(uc[:, 0:2], 0.0), W=[ucb])
                        else:
                            C.op('pool', CP(uc[:, 0:2], ccar[l][:, cidx, :]), R=[ccarb[l][cidx]], W=[ucb])
                        C.op('dve', STT(y[:, :], uc[:, 1:513], cw[l][:, 1, cidx:cidx + 1], y[:, :], ALU.mult, ALU.add),
                             R=[ucb, yb, cw_b[l]], W=[yb])
                        C.op('dve', STT(y[:, :], uc[:, 0:512], cw[l][:, 0, cidx:cidx + 1], y[:, :], ALU.mult, ALU.add),
                             R=[ucb, yb, cw_b[l]], W=[yb])
                        C.op('pool', CP(ccar[l][:, cidx, :], uc[:, 512:514]), R=[ucb], W=[ccarb[l][cidx]])
                        ys.append((y, yb))
                    sg, sgb = rot('f512')
                    C.op('act', ACT(sg[:, :], ys[0][0][:, :], AF.Silu), R=[ys[0][1]], W=[sgb])
                    C.op('dve', TT(actT_c[c], ys[1][0][:, :], sg[:, :], ALU.mult), R=[ys[1][1], sgb], W=[actTb[c]])
                    flush()
                w_load()

        def ffn_down_g(l, key, after_j, xi):
            xs, xb = XS[xi], XB[xi]
            n = 0
            bks = []
            for _ in range(NSUB):
                bk_ = yield from gbank()
                bks.append(bk_)
            for kg in range(3):
                kc = 8 if kg < 2 else 6
                wt, wb = w_acquire(l, 27 + n * 3 + kg)
                for j in range(NSUB):
                    ps, pb = bks[j]
                    C.op('pe', [MM(ps[:, :], actT_c[kg * 8 + k][:, j * 128:(j + 1) * 128], wt[:, k, :],
                                   kg == 0 and k == 0, kg == 2 and k == kc - 1) for k in range(kc)],
                         R=[actTb[kg * 8 + k] for k in range(kc)] + [wb], W=[pb], selfdep=(kg == 0))
                w_load()
            for j in range(NSUB):
                ps, pb = bks[j]
                xv = xs[:, j, n * 512:(n + 1) * 512]
                C.op('dve', TT(xv, ps[:, :], xv, ALU.add), R=[pb, xb[j]], W=[xb[j]])
                free(pb)
            n = 1
            tiles = [w_acquire(l, 27 + n * 3 + kg) for kg in range(3)]
            for j in range(NSUB):
                ps, pb = yield from gbank()
                fns = []
                Rl = []
                for kg in range(3):
                    kc = 8 if kg < 2 else 6
                    wt, wb = tiles[kg]
                    for k in range(kc):
                        fns.append(MM(ps[:, :], actT_c[kg * 8 + k][:, j * 128:(j + 1) * 128], wt[:, k, :],
                                      kg == 0 and k == 0, kg == 2 and k == kc - 1))
                    Rl += [actTb[kg * 8 + k] for k in range(kc)] + [wb]
                C.op('pe', fns, R=Rl, W=[pb])
                xv = xs[:, j, n * 512:(n + 1) * 512]
                C.op('dve', TT(xv, ps[:, :], xv, ALU.add), R=[pb, xb[j]], W=[xb[j]])
                free(pb)
                if after_j is not None:
                    after_j(j)
                xflags[(key, j)] = True
                yield
            for _ in range(3):
                w_load()

        for _ in range(NSLOT):
            w_load()
        if NST > 1:
            load_x(1)
        order = [(st, l) for st in range(NST) for l in range(2)]
        for idx, (st, l) in enumerate(order):
            seq, pos = st // nst_seq, (st % nst_seq) * T
            first = (pos == 0)
            xi = st % 2
            if idx == 0:
                norm_T(g1T[l], g1T_b[l], xi)
            if first:
                C.op('pool', MS(Sst[l][:, :, :], 0.0), W=[Sstb[l]])
                C.op('pool', MS(Sbf[l][0][:, :, :], 0.0), W=[Sbfb[l][0]])
            proj(l)
            if idx == 0:
                build_bias()
            flags.clear()
            A_ = [attention(l, j, not (first and j == 0)) for j in range(NSUB)]
            H_ = [hgrn(l, j) for j in range(NSUB)]
            pend = [H_[0], A_[0], H_[1], H_[2], A_[1], H_[3], A_[2], A_[3]]
            windowed(pend, 4, always=[gates(l)], always_every=1, always_start=FILL_START)
            merge(l)
            xflags.clear()
            windowed([out_proj_g(l, 'x2', xi)] + [norm_T_j(g2T[l], g2T_b[l], j, 'x2', xi) for j in range(NSUB)], 5)
            if st == 0 and l == 0:
                dump('x1', XS[xi][:, 0, :], [XB[xi][0]])
            ffn_up(l, first)
            nxt = order[idx + 1] if idx + 1 < len(order) else None
            after = None
            if l == 1:
                def after(j, st=st, seq=seq, pos=pos, xi=xi):
                    C.dma('sp', out=y_d[seq, pos + j * 128:pos + (j + 1) * 128, :], in_=XS[xi][:, j, :], R=[XB[xi][j]],
                          sem='xst%d_%d' % (xi, j))
                    if j == NSUB - 1 and st + 2 < NST:
                        load_x(st + 2)
            gens = [ffn_down_g(l, 'x1', after, xi)]
            if nxt is not None:
                ln = nxt[1]
                if l == 1:
                    for j in range(NSUB):
                        xflags[('x1', j)] = True
                    gens = [norm_T_j(g1T[ln], g1T_b[ln], j, 'x1', 1 - xi) for j in range(NSUB)] + gens
                else:
                    gens += [norm_T_j(g1T[ln], g1T_b[ln], j, 'x1', xi) for j in range(NSUB)]
            windowed(gens, 5)
        C.final_wait('sp', ['xst%d_%d' % (a, b) for a in range(2) for b in range(NSUB)] + ['dbg'])
        print('tracker: waits emitted', C.nwait)
        C.emit()
    return nc


_W_NAMES = ["norm1", "w_in", "q_norm", "k_norm", "sinks", "rel_bias", "hg_lb", "hg_norm", "w_pa", "w_ph",
            "w_out", "norm2", "w_up", "conv_w", "conv_b", "w_down"]


def kernel(**inputs):
    x = np.ascontiguousarray(np.asarray(inputs["x"], dtype=np.float32))
    Bt, S, _ = x.shape
    ncores = 8
    nseq = Bt // ncores
    nc = build(nseq, S)
    wts = {k: np.ascontiguousarray(np.asarray(inputs[k], dtype=np.float32)) for k in _W_NAMES}
    in_maps = []
    for c in range(ncores):
        m = {"x": x[c * nseq:(c + 1) * nseq]}
        m.update(wts)
        in_maps.append(m)
    res = run_bass_kernel_spmd(nc, in_maps, core_ids=list(range(ncores)))
    return np.concatenate([r["y"] for r in res.results], axis=0)
```

```python
from contextlib import ExitStack
import math
import numpy as np
import concourse.bass as bass
import concourse.mybir as mybir
from concourse.bass_utils import run_bass_kernel_spmd

F32 = mybir.dt.float32
BF16 = mybir.dt.bfloat16
ALU = mybir.AluOpType
AF = mybir.ActivationFunctionType
AX = mybir.AxisListType

D = 1024
DIN = 4864
DFF = 2816
NUP = 5632
T = 512
NSUB = 4
EPS = 1e-6
NSLOT = 4
FILL_START = 0
NEG = -30000.0
ENG = ('pe', 'act', 'dve', 'pool', 'sp')
HOP = 0.25
DEFC = {'pe': 0.22, 'act': 0.65, 'dve': 0.65, 'pool': 1.15, 'sp': 0.1}


def t5_ranges():
    d = np.arange(0, 128)
    dd = np.maximum(d, 1).astype(np.float32)
    large = 16 + (np.log(dd / np.float32(16)) / np.float32(math.log(128 / 16)) * np.float32(16)).astype(np.int32)
    large = np.minimum(large, 31)
    b = np.where(d < 16, d, large)
    return b


class Buf:
    __slots__ = ('w', 'r', 'const')

    def __init__(self, const=False):
        self.w = None
        self.r = {}
        self.const = const


class Ctx:
    def __init__(self, nc, stack):
        self.nc = nc
        self.stack = stack
        self.streams = {k: [] for k in ENG}
        self.sems = {}
        self.cnt = {}
        self.seen = {k: {} for k in ENG}
        for k in ENG:
            self.sems[k] = stack.enter_context(nc.semaphore("sem_" + k))
            self.cnt[k] = 0
        self.efree = {k: 0.0 for k in ENG}
        self.fin = {}
        self.step_end = 0.0
        self.nops = 0
        self.snap = {}
        self.nwait = 0

    def _ready_t(self, eng, R, W):
        t = 0.0
        for b in R:
            if b.w is not None:
                t = max(t, self.fin.get(b.w, 0.0) + (0.0 if b.w[0] == eng else HOP))
        for b in W:
            if b.w is not None:
                t = max(t, self.fin.get(b.w, 0.0) + (0.0 if b.w[0] == eng else HOP))
            for sv in b.r.items():
                t = max(t, self.fin.get(sv, 0.0) + (0.0 if sv[0] == eng else HOP))
        return t

    def _sem(self, name):
        if name not in self.sems:
            self.sems[name] = self.stack.enter_context(self.nc.semaphore("sem_" + name))
            self.cnt[name] = 0
        return name

    def _waits(self, eng, R, W, selfdep=True):
        need = {}
        for b in R:
            if b.w is not None:
                need[b.w[0]] = max(need.get(b.w[0], 0), b.w[1])
        for b in W:
            if b.w is not None:
                need[b.w[0]] = max(need.get(b.w[0], 0), b.w[1])
            for s, v in b.r.items():
                need[s] = max(need.get(s, 0), v)
        seen = self.seen[eng]
        for s, v in sorted(need.items(), key=lambda kv: -self.fin.get(kv, 0.0)):
            if s == eng and not selfdep:
                continue
            if seen.get(s, 0) < v:
                self.streams[eng].append(('w', s, v))
                self.nwait += 1
                seen[s] = v
                sn = self.snap.get((s, v))
                if sn:
                    for s2, v2 in sn.items():
                        if seen.get(s2, 0) < v2:
                            seen[s2] = v2

    def _mark(self, tk, R, W):
        for b in R:
            if not b.const:
                b.r[tk[0]] = max(b.r.get(tk[0], 0), tk[1])
        for b in W:
            b.w = tk
            b.r = {}

    def op(self, eng, fns, R=(), W=(), selfdep=True):
        rt = self._ready_t(eng, R, W)
        self._waits(eng, R, W, selfdep)
        if callable(fns):
            fns = [fns]
        self.cnt[eng] += 1
        tk = (eng, self.cnt[eng])
        end = max(rt, self.efree[eng]) + sum(getattr(f, 'c', DEFC[eng]) for f in fns) * (1.8 if eng == 'pool' else 1.0)
        self.efree[eng] = end
        self.fin[tk] = end
        self.step_end = max(self.step_end, end)
        self.nops += 1
        self.streams[eng].append(('o', fns, eng, 1))
        self.snap[tk] = dict(self.seen[eng])
        self._mark(tk, R, W)

    def dma(self, eng, out, in_, R=(), W=(), sem='d0', **kw):
        if sem.startswith('cst'):
            self.nuniq = getattr(self, 'nuniq', 0) + 1
            sem = 'cstu%d' % self.nuniq
        self._sem(sem)
        rt = self._ready_t(eng, R, W)
        self._waits(eng, R, W)
        self.cnt[sem] += 16
        tk = (sem, self.cnt[sem])
        st_ = max(rt, self.efree[eng])
        self.efree[eng] = st_ + 0.1
        self.fin[tk] = st_ + 4.0
        self.streams[eng].append(('o', [lambda e: e.dma_start(out=out, in_=in_, **kw)], sem, 16))
        if self.cnt[sem] == 16 or sem.startswith('wl') or sem.startswith('x'):
            self.snap[tk] = dict(self.seen[eng])
        self._mark(tk, R, W)

    def fix(self, sem, bufs):
        for b in bufs:
            b.w = (sem, self.cnt[sem])

    def final_wait(self, eng, sems):
        for s in sems:
            if s in self.cnt and self.cnt[s] > 0:
                self.streams[eng].append(('w', s, self.cnt[s]))

    def emit(self):
        nc = self.nc

        def run(key, e):
            for it in self.streams[key]:
                if it[0] == 'w':
                    e.wait_ge(self.sems[it[1]], it[2])
                else:
                    _, fns, s, inc = it
                    for f in fns[:-1]:
                        f(e)
                    ins = fns[-1](e)
                    ins.then_inc(self.sems[s], inc)

        with nc.Block() as block:
            @block.tensor
            def _(e):
                run('pe', e)

            @block.scalar
            def _(e):
                run('act', e)

            @block.vector
            def _(e):
                run('dve', e)

            @block.gpsimd
            def _(e):
                run('pool', e)

            @block.sync
            def _(e):
                run('sp', e)


def _n(ap):
    n = 1
    for d in list(ap.shape)[1:]:
        n *= int(d)
    return n


def _c(f, c):
    f.c = c
    return f


def MM(out, lhsT, rhs, start, stop):
    n = _n(rhs)
    c = max(0.03, n / 2400.0) * (4.0 if rhs.dtype == F32 else 1.0) + 0.005
    return _c(lambda e: e.matmul(out=out, lhsT=lhsT, rhs=rhs, start=start, stop=stop, skip_group_check=True), c)


def TR(out, in_, ident):
    return _c(lambda e: e.transpose(out=out, in_=in_, identity=ident), 0.1)


def ACT(out, in_, func, **kw):
    return _c(lambda e: e.activation(out=out, in_=in_, func=func, **kw), (_n(out) + 230) / 1200.0)


def TT(out, in0, in1, op):
    return _c(lambda e: e.tensor_tensor(out=out, in0=in0, in1=in1, op=op), (_n(out) + 100) / 960.0)


def TS(out, in0, s1, s2, op0, op1=None):
    c = (_n(out) + 100) / 960.0
    if op1 is None:
        return _c(lambda e: e.tensor_scalar(out=out, in0=in0, scalar1=s1, scalar2=None, op0=op0), c)
    return _c(lambda e: e.tensor_scalar(out=out, in0=in0, scalar1=s1, scalar2=s2, op0=op0, op1=op1), c)


def STT(out, in0, scalar, in1, op0, op1):
    return _c(lambda e: e.scalar_tensor_tensor(out=out, in0=in0, scalar=scalar, in1=in1, op0=op0, op1=op1),
              (_n(out) + 150) / 960.0)


def CP(out, in_):
    return _c(lambda e: e.tensor_copy(out=out, in_=in_), (_n(out) / 2 + 100) / 960.0)


def RED(out, in_):
    return _c(lambda e: e.tensor_reduce(out=out, in_=in_, axis=AX.X, op=ALU.add), (_n(in_) + 100) / 960.0)


def MS(ap, val):
    return _c(lambda e: e.memset(ap, val), (_n(ap) + 60) / 960.0)


def RCP(out, in_):
    return _c(lambda e: e.reciprocal(out=out, in_=in_), (_n(out) + 100) / 960.0)


def build(nseq, S, dbg=None):
    nc = bass.Bass("TRN2", target_bir_lowering=False)
    nst_seq = S // T
    NST = nseq * nst_seq
    with ExitStack() as stack:
        C = Ctx(nc, stack)

        def dram(name, shape, dt=F32, kind="ExternalInput"):
            return nc.dram_tensor(name, list(shape), dt, kind=kind)

        x_d = dram("x", [nseq, S, D]).ap()
        y_d = dram("y", [nseq, S, D], kind="ExternalOutput").ap()
        norm1_d = dram("norm1", [2, D])
        w_in_d = dram("w_in", [2, D, DIN]).ap()
        q_norm_d = dram("q_norm", [2, 64])
        k_norm_d = dram("k_norm", [2, 64])
        sinks_d = dram("sinks", [2, 8])
        rel_bias_d = dram("rel_bias", [32, 8]).ap()
        hg_lb_d = dram("hg_lb", [2, 512])
        hg_norm_d = dram("hg_norm", [2, 128])
        w_pa_d = dram("w_pa", [2, 512, D]).ap()
        w_ph_d = dram("w_ph", [2, 512, D]).ap()
        w_out_d = dram("w_out", [2, D, D]).ap()
        norm2_d = dram("norm2", [2, D])
        w_up_d = dram("w_up", [2, D, NUP]).ap()
        conv_w_d = dram("conv_w", [2, 3, NUP]).ap()
        conv_b_d = dram("conv_b", [2, NUP]).ap()
        w_down_d = dram("w_down", [2, DFF, D]).ap()
        dbg_out = {}
        if dbg:
            for name, shape in dbg.items():
                dbg_out[name] = dram("dbg_" + name, shape, kind="ExternalOutput").ap()

        def sb(name, shape, dt=F32):
            return stack.enter_context(nc.sbuf_tensor(name, list(shape), dt))

        POOLS = {'f512': ([128, 512], F32, 8), 'b512': ([128, 512], BF16, 15), 'b1024': ([128, 1024], BF16, 4),
                 's8': ([128, 8], F32, 24), 'f128': ([128, 128], F32, 4), 'ucat': ([128, 514], F32, 3)}
        _pools = {}

        class Scope:
            def __init__(self):
                self.items = []

        cur = [Scope()]

        def rot(name):
            if name not in _pools:
                shape, dt, n = POOLS[name]
                _pools[name] = [(sb("%s_%d" % (name, i), shape, dt), Buf()) for i in range(n)]
            assert _pools[name], "scratch pool %s exhausted" % name
            item = _pools[name].pop(0)
            cur[0].items.append((name, item))
            return item

        def free(*bufs):
            for b in bufs:
                for ent in cur[0].items:
                    if ent[1][1] is b:
                        cur[0].items.remove(ent)
                        _pools[ent[0]].append(ent[1])
                        break
                else:
                    raise AssertionError("free of unowned tile")

        def flush(scope=None):
            sc = scope or cur[0]
            for name, item in sc.items:
                _pools[name].append(item)
            sc.items = []

        def gstep(gs):
            prev = cur[0]
            cur[0] = gs[1]
            try:
                next(gs[0])
                alive = True
            except StopIteration:
                flush(gs[1])
                alive = False
            cur[0] = prev
            return alive

        _pools['bank'] = [(stack.enter_context(nc.psum_tensor("ps%d" % i, [128, 512], F32)), Buf()) for i in range(8)]

        def bank():
            return rot('bank')

        def gbank():
            while not _pools['bank']:
                yield
            return rot('bank')

        A_chunks = [(0, 512), (512, 768), (768, 1280), (1280, 1792), (1792, 2304), (2304, 2816)]
        B_chunks = [(2816, 3328), (3328, 3840), (3840, 4352), (4352, 4864)]
        def tile_list(l):
            tl = []
            for (c0, c1) in A_chunks + B_chunks:
                tl.append((8, c1 - c0, [(0, w_in_d[l, :, c0:c1])]))
            for n in range(2):
                tl.append((4, 512, [(0, w_pa_d[l, :, n * 512:(n + 1) * 512])]))
                tl.append((4, 512, [(0, w_ph_d[l, :, n * 512:(n + 1) * 512])]))
            for n in range(2):
                tl.append((8, 512, [(0, w_out_d[l, :, n * 512:(n + 1) * 512])]))
            for i in range(11):
                tl.append((8, 512, [(0, w_up_d[l, :, 256 * i:256 * i + 256]),
                                    (256, w_up_d[l, :, DFF + 256 * i:DFF + 256 * i + 256])]))
            for n in range(2):
                for kg in range(3):
                    kc = 8 if kg < 2 else 6
                    tl.append((kc, 512, [(0, w_down_d[l, kg * 1024:kg * 1024 + kc * 128, n * 512:(n + 1) * 512])]))
            return tl

        tls = [tile_list(l) for l in range(2)]
        NTL = len(tls[0])
        wsc = [dram("wsc%d" % l, [NTL, 128, 4096], BF16, kind="Internal").ap() for l in range(2)]
        wsc_b = [[Buf(const=True) for _ in range(NTL)] for l in range(2)]

        def wsc_view(l, i):
            kc, w, _ = tls[l][i]
            return wsc[l][i][:, 0:kc * w].rearrange("p (k n) -> p k n", k=kc)

        NGRP = (NTL + 3) // 4
        cast_done = [0]

        def cast_next():
            gi = cast_done[0]
            if gi >= 2 * NGRP:
                return
            cast_done[0] += 1
            l, g = gi // NGRP, gi % NGRP
            sem = 'wc%d_%d' % (l, g)
            grp = []
            for i in range(g * 4, min(g * 4 + 4, NTL)):
                kc, w, parts = tls[l][i]
                v = wsc_view(l, i)
                for (d0, src) in parts:
                    wd = src.shape[1]
                    C.dma('pool', out=v[:, :, d0:d0 + wd], in_=src.rearrange("(k p) n -> p k n", p=128),
                          W=[wsc_b[l][i]], sem=sem)
                grp.append(wsc_b[l][i])
            C.fix(sem, grp)

        CAST_AHEAD = 3

        slots = [(sb("wslot%d" % i, [128, 8, 512], BF16), Buf()) for i in range(NSLOT)]
        wseq_total = NST * 2 * NTL
        wstate = {'load': 0, 'cons': 0}

        def w_load():
            n = wstate['load']
            if n >= wseq_total:
                return
            wstate['load'] += 1
            l = (n // NTL) % 2
            i = n % NTL
            if n < 2 * NTL:
                while cast_done[0] < min(2 * NGRP, l * NGRP + i // 4 + 1 + CAST_AHEAD):
                    cast_next()
            kc, w, _ = tls[l][i]
            st_, sbuf_ = slots[n % NSLOT]
            C.dma('sp', out=st_[:, 0:kc, 0:w], in_=wsc_view(l, i), R=[wsc_b[l][i]], W=[sbuf_],
                  sem='wl%d' % (n % NSLOT))

        def w_acquire(l_expect, i_expect):
            n = wstate['cons']
            assert (n // NTL) % 2 == l_expect and n % NTL == i_expect, (n, l_expect, i_expect)
            wstate['cons'] += 1
            return slots[n % NSLOT]

        xs = sb("xs", [128, NSUB, D]); xb = [Buf() for _ in range(NSUB)]
        xs1 = sb("xs1", [128, NSUB, D]); xb1 = [Buf() for _ in range(NSUB)]
        XS = [xs, xs1]; XB = [xb, xb1]
        hT = sb("hT", [128, 8, T], BF16); hTb = [Buf() for _ in range(NSUB)]
        sigT = sb("sigT", [128, 16, T], BF16); sigTb = [Buf() for _ in range(16)]
        aT = sb("aT", [128, 4, T], BF16); aTb = [Buf() for _ in range(NSUB)]
        rT = sb("rT", [128, 4, T], BF16); rTb = [Buf() for _ in range(NSUB)]
        qn = [sb("qn%d" % j, [128, 512], BF16) for j in range(NSUB)]; qnb = [Buf() for _ in range(NSUB)]
        kn = [sb("kn%d" % j, [128, 128], BF16) for j in range(NSUB)]; knb = [Buf() for _ in range(NSUB)]
        sq = [sb("sq%d" % j, [128, 512], BF16) for j in range(NSUB)]; sqb = [Buf() for _ in range(NSUB)]
        kk = [sb("kk%d" % j, [128, 512]) for j in range(NSUB)]; kkb = [Buf() for _ in range(NSUB)]
        kkh = [sb("kkh%d" % j, [128, 512], BF16) for j in range(NSUB)]; kkhb = [Buf() for _ in range(NSUB)]
        iv = [sb("iv%d" % j, [128, 512], BF16) for j in range(NSUB)]; ivb = [Buf() for _ in range(NSUB)]
        gsil = [sb("gsil%d" % j, [128, 512], BF16) for j in range(NSUB)]; gsilb = [Buf() for _ in range(NSUB)]
        actT_c = [sigT[:, i, :] for i in range(16)] + [sq[j][:, :] for j in range(NSUB)] + [iv[0][:, :], iv[1][:, :]]
        actTb = sigTb + sqb + [ivb[0], ivb[1]]
        KT = [[sb("KT%d_%d" % (l, j), [128, 256], BF16) for j in range(5)] for l in range(2)]
        KTb = [[Buf() for _ in range(5)] for l in range(2)]
        for l in range(2):
            for j in range(5):
                C.op('pool', MS(KT[l][j][:, :], 0.0), W=[KTb[l][j]])
        V = [[sb("V%d_%d" % (l, j), [128, 2, 65], BF16) for j in range(5)] for l in range(2)]
        Vb = [[Buf() for _ in range(5)] for l in range(2)]
        for l in range(2):
            for j in range(5):
                C.op('pool', MS(V[l][j][:, :, 64:65], 1.0), W=[Vb[l][j]])
        Sst = [sb("S%d" % l, [128, 4, 128]) for l in range(2)]; Sstb = [Buf() for _ in range(2)]
        Sbf = [[sb("Sbf%d_%d" % (l, i), [128, 4, 128], BF16) for i in range(2)] for l in range(2)]
        Sbfb = [[Buf() for _ in range(2)] for _ in range(2)]
        ccar = [sb("ccar%d" % l, [128, 44, 2]) for l in range(2)]
        ccarb = [[Buf() for _ in range(44)] for l in range(2)]

        def load_x(st):
            seq, pos = st // nst_seq, (st % nst_seq) * T
            xi = st % 2
            for j in range(NSUB):
                C.dma('sp', out=XS[xi][:, j, :], in_=x_d[seq, pos + j * 128:pos + (j + 1) * 128, :], W=[XB[xi][j]],
                      sem='xl%d_%d' % (xi, j))

        load_x(0)

        def cbuf():
            return Buf(const=True)

        ones_f = sb("ones_f", [128, 128]); ones_b = cbuf()
        identf = sb("identf", [128, 128]); identf_b = cbuf()
        ident = sb("ident", [128, 128], BF16); ident_b = cbuf()
        U = sb("U", [128, 128]); U_b = cbuf()
        maskU4 = sb("maskU4", [128, 4, 128], BF16); maskU4_b = cbuf()
        chunkind = sb("chunkind", [128, 2]); chunkind_b = cbuf()
        neghalf = sb("neghalf", [128, 8]); neghalf_b = cbuf()

        C.op('pool', MS(ones_f[:, :], 1.0), W=[ones_b])
        C.op('pool', lambda e: e.affine_select(out=identf[:, :], in_=ones_f[:, :], pattern=[[-1, 128]],
                                                compare_op=ALU.is_equal, fill=0.0, base=0, channel_multiplier=1),
             R=[ones_b], W=[identf_b])
        C.op('dve', CP(ident[:, :], identf[:, :]), R=[identf_b], W=[ident_b])
        antif = sb("antif", [128, 128]); antif_b = Buf()
        anti = sb("anti", [128, 128], BF16); anti_b = cbuf()
        C.op('pool', lambda e: e.affine_select(out=antif[:, :], in_=ones_f[:, :], pattern=[[1, 128]],
                                                compare_op=ALU.is_equal, fill=0.0, base=-127, channel_multiplier=1),
             R=[ones_b], W=[antif_b])
        C.op('dve', CP(anti[:, :], antif[:, :]), R=[antif_b], W=[anti_b])
        C.op('pool', lambda e: e.affine_select(out=U[:, :], in_=ones_f[:, :], pattern=[[1, 128]],
                                                compare_op=ALU.is_ge, fill=0.0, base=0, channel_multiplier=-1),
             R=[ones_b], W=[U_b])
        C.op('pool', MS(U[0:64, 64:128], 0.0), W=[U_b])
        for h in range(4):
            C.op('dve', CP(maskU4[:, h, :], U[:, :]), R=[U_b], W=[maskU4_b])
        C.op('pool', MS(chunkind[:, :], 0.0), W=[chunkind_b])
        C.op('pool', MS(chunkind[0:64, 0:1], 1.0), W=[chunkind_b])
        C.op('pool', MS(chunkind[64:128, 1:2], 1.0), W=[chunkind_b])
        C.op('pool', MS(neghalf[:, :], -0.5), W=[neghalf_b])

        def bc_src(th, off, ap):
            return bass.AP(th, off, ap)

        hgn_bc, gqk_bc, oml_bc, esink, g1T, g2T, cw = [], [], [], [], [], [], []
        hgn_b, gqk_b, oml_b, esink_b, g1T_b, g2T_b, cw_b = [], [], [], [], [], [], []
        lb_bc = xs1[:, 0, :].rearrange("p (a n) -> p a n", a=2); lb_bc_b = xb1[0]
        C.dma('sp', out=lb_bc, in_=bc_src(hg_lb_d, 0, [[0, 128], [512, 2], [1, 512]]), W=[lb_bc_b], sem='cst1')
        for l in range(2):
            t_ = sb("hgn_bc%d" % l, [128, 128]); b_ = cbuf()
            C.dma('sp', out=t_[:, :], in_=bc_src(hg_norm_d, l * 128, [[0, 128], [1, 128]]), W=[b_], sem='cst2')
            hgn_bc.append(t_); hgn_b.append(b_)
            gq = sb("gq_bc%d" % l, [128, 2, 64]); gqb = Buf()
            gk = sb("gk_bc%d" % l, [128, 2, 64]); gkb = Buf()
            C.dma('sp', out=gq[:, :, :], in_=bc_src(q_norm_d, l * 64, [[0, 128], [0, 2], [1, 64]]), W=[gqb], sem='cst3')
            C.dma('sp', out=gk[:, :, :], in_=bc_src(k_norm_d, l * 64, [[0, 128], [0, 2], [1, 64]]), W=[gkb], sem='cst4')
            t_ = sb("gqk_bc%d" % l, [128, 2, 64]); b_ = cbuf()
            C.op('dve', TT(t_[:, :, :], gq[:, :, :], gk[:, :, :], ALU.mult), R=[gqb, gkb], W=[b_])
            gqk_bc.append(t_); gqk_b.append(b_)
            if l == 0:
                t_ = None; b_ = None
            else:
                t_ = sb("oml_bc%d" % l, [128, 512]); b_ = cbuf()
                dl = xs1[:, 1, 0:512]; dlb = xb1[1]
                C.op('dve', TT(dl, lb_bc[:, 0, :], lb_bc[:, 1, :], ALU.subtract), R=[lb_bc_b], W=[dlb])
                C.op('act', ACT(t_[:, :], dl, AF.Sigmoid), R=[dlb], W=[b_])
            oml_bc.append(t_); oml_b.append(b_)
            sk = sb("sink_bc%d" % l, [128, 8]); skb = Buf()
            C.dma('sp', out=sk[:, :], in_=bc_src(sinks_d, l * 8, [[0, 128], [1, 8]]), W=[skb], sem='cst5')
            t_ = sb("esink%d" % l, [128, 8]); b_ = cbuf()
            C.op('act', ACT(t_[:, :], sk[:, :], AF.Exp), R=[skb], W=[b_])
            esink.append(t_); esink_b.append(b_)
            for (lst, lstb, src, nm) in ((g1T, g1T_b, norm1_d, "g1T"), (g2T, g2T_b, norm2_d, "g2T")):
                t_ = sb("%s%d" % (nm, l), [128, 8]); b_ = cbuf()
                C.dma('sp', out=t_[:, :], in_=bc_src(src, l * D, [[1, 128], [128, 8]]), W=[b_], sem='cst6',
                      allow_slow_non_contiguous=True)
                lst.append(t_); lstb.append(b_)
            stg = xs1[0:44, 2 + l, 0:512].rearrange("p (a n) -> p a n", a=4); stgb = xb1[2 + l]
            for j in range(3):
                C.dma('sp', out=stg[:, j, :], in_=conv_w_d[l, j, :].rearrange("(c p) -> c p", p=128), W=[stgb], sem='cst7')
            C.dma('sp', out=stg[:, 3, :], in_=conv_b_d[l, :].rearrange("(c p) -> c p", p=128), W=[stgb], sem='cst8')
            t_ = sb("cw%d" % l, [128, 4, 44]); b_ = cbuf()
            ps, pb = bank()
            C.op('pe', [TR(ps[:, j * 44:(j + 1) * 44], stg[:, j, :], identf[0:44, 0:44]) for j in range(4)],
                 R=[stgb, identf_b], W=[pb])
            C.op('dve', CP(t_[:, :, :], ps[:, 0:176].rearrange("p (j c) -> p j c", j=4)), R=[pb], W=[b_])
            flush()
            cw.append(t_); cw_b.append(b_)

        bk = t5_ranges()
        rbT = sb("rbT", [8, 32]); rbT_b = Buf()
        C.dma('sp', out=rbT[:, :], in_=rel_bias_d.rearrange("b h -> h b"), W=[rbT_b], sem='cst9',
              allow_slow_non_contiguous=True)
        biasT = {}
        biasT_b = cbuf()

        def build_bias():
            e_sb = sb("e_sb", [8, 384]); e_sbb = Buf()
            C.op('dve', MS(e_sb[:, :], NEG), W=[e_sbb])
            C.op('dve', CP(e_sb[:, 127:143], rbT[:, 0:16]), R=[rbT_b], W=[e_sbb])
            for b in range(16, 32):
                idx = np.nonzero(bk == b)[0]
                if len(idx) == 0:
                    continue
                lo, hi = int(idx[0]), int(idx[-1]) + 1
                assert np.all(bk[lo:hi] == b)
                C.op('dve', CP(e_sb[:, 127 + lo:127 + hi], rbT[:, b:b + 1].to_broadcast([8, hi - lo])), R=[rbT_b], W=[e_sbb])
            e_d = dram("e_scr", [8, 384], kind="Internal")
            e_db = Buf()
            C.dma('sp', out=e_d.ap(), in_=e_sb[:, :], R=[e_sbb], W=[e_db], sem='cst10')
            for hk in range(2):
                for kind, off in (('prev', 255), ('cur', 127)):
                    t_ = sb("biasT_%d%s" % (hk, kind), [128, 4, 128], BF16)
                    stg_, stgfb = rot('f512')
                    stgf = stg_[:, :].rearrange("p (g q) -> p g q", g=4)
                    for g in range(4):
                        h = hk * 4 + g
                        C.dma('sp', out=stgf[:, g, :], in_=bc_src(e_d, h * 384 + off - 127, [[1, 128], [1, 128]]),
                              R=[e_db], W=[stgfb], sem='cst11')
                    C.op('dve', CP(t_[:, :, :], stgf), R=[stgfb], W=[biasT_b])
                    biasT[(hk, kind)] = t_
            flush()

        def dump(name, ap, bufs):
            if name in dbg_out:
                C.dma('sp', out=dbg_out[name], in_=ap, R=bufs, sem='dbg')

        def windowed(pending, W, always=(), always_every=1, always_start=0):
            pending = [[g, Scope(), 0.0] for g in pending]
            active = []
            for g in always:
                active.append([g, Scope(), 0.0, True])
            nalways = len(active)
            while pending or active:
                while pending and sum(1 for a in active if len(a) == 3) < W:
                    ent = pending.pop(0)
                    ent[2] = min([a[2] for a in active] + [C.efree['pe']])
                    active.append(ent)
                ent = min(active, key=lambda a: a[2])
                C.step_end = 0.0
                n0 = C.nops
                alive = gstep(ent)
                if not alive:
                    active.remove(ent)
                elif C.nops == n0:
                    others = sorted(a[2] for a in active if a is not ent)
                    ent[2] = (others[0] if others else ent[2]) + 1e-3
                    if others and all(abs(o - others[0]) < 1e-2 for o in others) and len(others) > 0:
                        ent[2] = others[-1] + 1e-3
                else:
                    ent[2] = C.step_end

        def interleave(gens):
            windowed(list(gens), 99)

        flags = {}

        xflags = {}

        def norm_T(gT, gTb, xi=0):
            interleave([norm_T_j(gT, gTb, j, None, xi) for j in range(NSUB)])

        def norm_T_j(gT, gTb, j, key=None, xi=0):
            xs, xb = XS[xi], XB[xi]
            while key is not None and not xflags.get((key, j)):
                yield
            if True:
                junk, junkb = rot('b1024')
                ssq, ssqb = rot('s8')
                C.op('act', ACT(junk[:, :], xs[:, j, :], AF.Square, accum_out=ssq[:, 0:1]), R=[xb[j]], W=[junkb, ssqb])
                free(junkb)
                yield
                ms, msb = rot('s8')
                C.op('pool', TS(ms[:, 0:1], ssq[:, 0:1], 1.0 / D, EPS, ALU.mult, ALU.add), R=[ssqb], W=[msb])
                rstd, rstdb = rot('s8')
                C.op('pool', TT(rstd[:, 0:1], ms[:, 0:1], neghalf[:, 0:1], ALU.pow), R=[msb, neghalf_b], W=[rstdb])
                yield
                hb, hbb = rot('b1024')
                C.op('act', ACT(hb[:, :], xs[:, j, :], AF.Copy, scale=rstd[:, 0:1]), R=[xb[j], rstdb], W=[hbb])
                yield
                ps, pb = yield from gbank()
                psv = ps[:, :].bitcast(BF16)
                C.op('pe', [TR(psv[:, k * 128:(k + 1) * 128], hb[:, k * 128:(k + 1) * 128], ident[:, :]) for k in range(8)],
                     R=[hbb, ident_b], W=[pb])
                free(hbb)
                yield
                for half in range(2):
                    C.op('dve', TT(hT[:, 4 * half:4 * half + 4, j * 128:(j + 1) * 128],
                                   psv[:, half * 512:(half + 1) * 512].rearrange("p (k t) -> p k t", k=4),
                                   gT[:, 4 * half:4 * half + 4].unsqueeze(2).to_broadcast([128, 4, 128]), ALU.mult),
                         R=[pb, gTb], W=[hTb[j]])

        def rsq_small(src, n, scale, epsv, name):
            ms, msb = rot('s8')
            C.op('dve', TS(ms[:, 0:n], src[0][:, 0:n], scale, epsv, ALU.mult, ALU.add), R=[src[1]], W=[msb])
            r, rb = rot('s8')
            C.op('pool', TT(r[:, 0:n], ms[:, 0:n], neghalf[:, 0:n], ALU.pow), R=[msb, neghalf_b], W=[rb])
            return r, rb

        def evac_q(l, j, ps, pb):
            qf, qfb = rot('f512')
            C.op('act', ACT(qf[:, :], ps[:, :], AF.Copy), R=[pb], W=[qfb])
            sqt, sqtb = rot('f512')
            C.op('dve', TT(sqt[:, :], qf[:, :], qf[:, :], ALU.mult), R=[qfb], W=[sqtb])
            ssq8, ssq8b = rot('s8')
            C.op('dve', RED(ssq8[:, :], sqt[:, :].rearrange("p (h d) -> p h d", h=8)), R=[sqtb], W=[ssq8b])
            rq, rqb = rsq_small((ssq8, ssq8b), 8, 1.0, 64 * EPS, 'rq')
            C.op('dve', TT(qn[j][:, :].rearrange("p (q g d) -> p q g d", q=4, g=2),
                           qf[:, :].rearrange("p (g q d) -> p q g d", g=2, q=4),
                           rq[:, 0:8].rearrange("p (g q) -> p q g", g=2).unsqueeze(3).to_broadcast([128, 4, 2, 64]), ALU.mult),
                 R=[qfb, rqb], W=[qnb[j]])

        def evac_kv(l, j, ps, pb):
            kf, kfb = rot('f128')
            C.op('act', ACT(kf[:, :], ps[:, 0:128], AF.Copy), R=[pb], W=[kfb])
            C.op('act', ACT(V[l][j][:, :, 0:64], ps[:, 128:256].rearrange("p (h d) -> p h d", h=2), AF.Copy),
                 R=[pb], W=[Vb[l][j]])
            sqt, sqtb = rot('f128')
            C.op('dve', TT(sqt[:, :], kf[:, :], kf[:, :], ALU.mult), R=[kfb], W=[sqtb])
            ssq2, ssq2b = rot('s8')
            C.op('dve', RED(ssq2[:, 0:2], sqt[:, :].rearrange("p (h d) -> p h d", h=2)), R=[sqtb], W=[ssq2b])
            rk, rkb = rsq_small((ssq2, ssq2b), 2, 1.0 / 64, EPS, 'rk')
            kt, ktb = rot('f128')
            C.op('dve', TT(kt[:, :].rearrange("p (h d) -> p h d", h=2), kf[:, :].rearrange("p (h d) -> p h d", h=2),
                           rk[:, 0:2].unsqueeze(2).to_broadcast([128, 2, 64]), ALU.mult), R=[kfb, rkb], W=[ktb])
            C.op('dve', TT(kn[j][:, :].rearrange("p (h d) -> p h d", h=2), kt[:, :].rearrange("p (h d) -> p h d", h=2),
                            gqk_bc[l][:, :, :], ALU.mult), R=[ktb, gqk_b[l]], W=[knb[j]])

        def evac_qr(l, j, ps, pb):
            C.op('act', ACT(sq[j][:, :], ps[:, :], AF.Silu), R=[pb], W=[sqb[j]])

        def evac_f(l, j, ps, pb):
            if oml_bc[l] is None:
                C.op('act', ACT(kk[j][:, :], ps[:, :], AF.Sigmoid, scale=-1.0), R=[pb], W=[kkb[j]])
            else:
                sn, snb = rot('f512')
                C.op('act', ACT(sn[:, :], ps[:, :], AF.Sigmoid, scale=-1.0), R=[pb], W=[snb])
                C.op('dve', TT(kk[j][:, :], sn[:, :], oml_bc[l][:, :], ALU.mult), R=[snb, oml_b[l]], W=[kkb[j]])
            C.op('dve', CP(kkh[j][:, :], kk[j][:, :]), R=[kkb[j]], W=[kkhb[j]])

        def logf_batch():
            for j in range(NSUB):
                C.op('act', ACT(kk[j][:, :], kk[j][:, :], AF.Ln, scale=-1.0, bias=1.0), R=[kkb[j]], W=[kkb[j]])

        def evac_i(l, j, ps, pb):
            C.op('dve', CP(iv[j][:, :], ps[:, :]), R=[pb], W=[ivb[j]])

        def evac_g(l, j, ps, pb):
            gs, gsb = rot('f512')
            C.op('act', ACT(gs[:, :], ps[:, :], AF.Silu), R=[pb], W=[gsb])
            C.op('pool', TT(gsil[j][:, :].rearrange("p (h d) -> p h d", h=4), gs[:, :].rearrange("p (h d) -> p h d", h=4),
                            hgn_bc[l][:, :].unsqueeze(1).to_broadcast([128, 4, 128]), ALU.mult),
                 R=[gsb, hgn_b[l]], W=[gsilb[j]])

        evacA = [evac_q, evac_kv, evac_qr, evac_f, evac_i, evac_g]

        def proj(l):
            for c, (c0, c1) in enumerate(A_chunks):
                wt, wb = w_acquire(l, c)
                w = c1 - c0
                for j in range(NSUB):
                    ps, pb = bank()
                    C.op('pe', [MM(ps[:, 0:w], hT[:, k, j * 128:(j + 1) * 128], wt[:, k, 0:w], k == 0, k == 7)
                                for k in range(8)], R=[hTb[j], wb], W=[pb])
                    evacA[c](l, j, ps, pb)
                    flush()
                if c == 3:
                    logf_batch()
                w_load()

        def gates(l):
            for t in range(4):
                wt, wb = w_acquire(l, 6 + t)
                for ii in range(4):
                    i = t * 4 + ii
                    ps, pb = yield from gbank()
                    C.op('pe', [MM(ps[:, :], wt[:, k, ii * 128:(ii + 1) * 128], hT[:, k, :], k == 0, k == 7)
                                for k in range(8)], R=hTb + [wb], W=[pb])
                    C.op('act', ACT(sigT[:, i, :], ps[:, :], AF.Tanh, scale=0.5), R=[pb], W=[sigTb[i]])
                    free(pb)
                    if ii == 3:
                        w_load()
                    yield

        def attention(l, j, has_prev):
            pj = 4 if j == 0 else j - 1
            ps, pb = yield from gbank()
            psv = ps[:, :].bitcast(BF16)
            C.op('pe', [TR(psv[:, p * 128:(p + 1) * 128], qn[j][:, p * 128:(p + 1) * 128], ident[:, :]) for p in range(4)],
                 R=[qnb[j], ident_b], W=[pb])
            QT, QTb = rot('b512')
            C.op('act', ACT(QT[:, :], psv[:, 0:512], AF.Copy), R=[pb], W=[QTb])
            free(pb)
            ps2, pb2 = yield from gbank()
            psv2 = ps2[:, :].bitcast(BF16)
            C.op('pe', [TR(psv2[:, 0:128], kn[j][:, :], ident[:, :])], R=[knb[j], ident_b], W=[pb2])
            C.op('dve', CP(KT[l][j][0:64, 0:128], psv2[0:64, 0:128]), R=[pb2], W=[KTb[l][j]])
            C.op('dve', CP(KT[l][j][64:128, 128:256], psv2[64:128, 0:128]), R=[pb2], W=[KTb[l][j]])
            free(pb2)
            flags[('kt', j)] = True
            yield
            while j > 0 and not flags.get(('kt', j - 1)):
                yield
            kinds = ([('prev', pj)] if has_prev else []) + [('cur', j)]
            PT = {}
            for hk in range(2):
                for kind, jj in kinds:
                    ps, pb = yield from gbank()
                    C.op('pe', [MM(ps[:, :], KT[l][jj][:, hk * 128:(hk + 1) * 128], QT[:, :], True, False),
                                MM(ps[:, :], anti[:, :], biasT[(hk, kind)][:, :, :].rearrange("p g q -> p (g q)"), False, True)],
                         R=[KTb[l][jj], QTb, anti_b, biasT_b], W=[pb])
                    pt, ptb = rot('b512')
                    C.op('act', ACT(pt[:, :], ps[:, :], AF.Exp), R=[pb], W=[ptb])
                    free(pb)
                    PT[(hk, kind)] = (pt, ptb)
                yield
            free(QTb)
            aout, aoutb = rot('b512')
            yield
            for half in range(2):
                hk = half
                ps, pb = yield from gbank()
                fns = []
                Rl = []
                for g in range(4):
                    for idx, (kind, jj) in enumerate(kinds):
                        pt, ptb = PT[(hk, kind)]
                        fns.append(MM(ps[:, g * 65:(g + 1) * 65], pt[:, g * 128:(g + 1) * 128], V[l][jj][:, hk, :],
                                      idx == 0, idx == len(kinds) - 1))
                        Rl += [ptb, Vb[l][jj]]
                C.op('pe', fns, R=Rl, W=[pb])
                free(*[PT[(hk, kind)][1] for kind, _ in kinds])
                psv3 = ps[:, 0:260].rearrange("p (g d) -> p g d", g=4)
                den, denb = rot('s8')
                C.op('dve', TT(den[:, 0:4].unsqueeze(2), psv3[:, :, 64:65], esink[l][:, half * 4:half * 4 + 4].unsqueeze(2), ALU.add),
                     R=[pb, esink_b[l]], W=[denb])
                rden, rdenb = rot('s8')
                C.op('dve', RCP(rden[:, 0:4], den[:, 0:4]), R=[denb], W=[rdenb])
                C.op('dve', TT(aout[:, half * 256:(half + 1) * 256].rearrange("p (g d) -> p g d", g=4), psv3[:, :, 0:64],
                               rden[:, 0:4].unsqueeze(2).to_broadcast([128, 4, 64]), ALU.mult), R=[pb, rdenb], W=[aoutb])
                free(denb, rdenb)
                free(pb)
                yield
            ps, pb = yield from gbank()
            psv = ps[:, :].bitcast(BF16)
            C.op('pe', [TR(psv[:, c * 128:(c + 1) * 128], aout[:, c * 128:(c + 1) * 128], ident[:, :]) for c in range(4)],
                 R=[aoutb, ident_b], W=[pb])
            free(aoutb)
            C.op('act', ACT(aT[:, :, j * 128:(j + 1) * 128], psv[:, 0:512].rearrange("p (c t) -> p c t", c=4), AF.Copy),
                 R=[pb], W=[aTb[j]])
            if j == NSUB - 1:
                C.op('pool', CP(KT[l][4][:, :], KT[l][3][:, :]), R=[KTb[l][3]], W=[KTb[l][4]])
                C.op('pool', CP(V[l][4][:, :, 0:64], V[l][3][:, :, 0:64]), R=[Vb[l][3]], W=[Vb[l][4]])
            if l == 0 and j == 1:
                dump('aout', aout[:, :], [aoutb])

        def hgrn(l, j):
            lf, lfb = kk[j], kkb[j]
            ps_b, pb_b = yield from gbank()
            C.op('pe', [MM(ps_b[:, :], U[:, :], lf[:, :], True, True)], R=[U_b, lfb], W=[pb_b])
            ps_l, pb_l = yield from gbank()
            C.op('pe', [MM(ps_l[:, h * 2:(h + 1) * 2], lf[:, h * 128:(h + 1) * 128], chunkind[:, :], True, True) for h in range(4)],
                 R=[lfb, chunkind_b], W=[pb_l])
            ebl, eblb = rot('s8')
            C.op('act', ACT(ebl[:, :], ps_l[:, 0:8], AF.Exp), R=[pb_l], W=[eblb])
            free(pb_l)
            yield
            eb, ebb = rot('f512')
            C.op('act', ACT(eb[:, :], ps_b[:, :], AF.Exp), R=[pb_b], W=[ebb])
            enb, enbb = rot('f512')
            C.op('act', ACT(enb[:, :], ps_b[:, :], AF.Exp, scale=-1.0), R=[pb_b], W=[enbb])
            free(pb_b)
            qp, qpb = rot('b512')
            C.op('dve', TT(qp[:, :], sq[j][:, :], eb[:, :], ALU.mult), R=[sqb[j], ebb], W=[qpb])
            free(ebb)
            kp, kpb = rot('b512')
            C.op('dve', TT(kp[:, :], kkh[j][:, :], enb[:, :], ALU.mult), R=[kkhb[j], enbb], W=[kpb])
            free(enbb)
            yield
            ps_t, pb_t = yield from gbank()
            psv = ps_t[:, :].bitcast(BF16)
            C.op('pe', [TR(psv[:, h * 128:(h + 1) * 128], qp[:, h * 128:(h + 1) * 128], ident[:, :]) for h in range(4)] +
                 [TR(psv[:, 512 + h * 128:512 + (h + 1) * 128], kp[:, h * 128:(h + 1) * 128], ident[:, :]) for h in range(4)],
                 R=[qpb, kpb, ident_b], W=[pb_t])
            free(qpb)
            qkT, qkTb = rot('b1024')
            C.op('act', ACT(qkT[:, :], psv[:, :], AF.Copy), R=[pb_t], W=[qkTb])
            free(pb_t)
            yield
            ps_a, pb_a = yield from gbank()
            C.op('pe', [MM(ps_a[:, h * 128:(h + 1) * 128], qkT[:, 512 + h * 128:512 + (h + 1) * 128],
                           qkT[:, h * 128:(h + 1) * 128], True, True) for h in range(4)], R=[qkTb], W=[pb_a])
            attm, attmb = rot('b512')
            C.op('dve', TT(attm[:, :], ps_a[:, :], maskU4[:, :, :].rearrange("p h t -> p (h t)"), ALU.mult),
                 R=[pb_a, maskU4_b], W=[attmb])
            free(pb_a)
            yield

            def hv(ap):
                return ap.rearrange("p (h v) -> p h v", h=4)

            ps_m0, pb_m0 = yield from gbank()
            C.op('pe', [MM(ps_m0[:, h * 128:(h + 1) * 128], kp[0:64, h * 128:(h + 1) * 128],
                           iv[j][0:64, h * 128:(h + 1) * 128], True, True) for h in range(4)],
                 R=[kpb, ivb[j]], W=[pb_m0])
            ps_m1, pb_m1 = yield from gbank()
            C.op('pe', [MM(ps_m1[:, h * 128:(h + 1) * 128], kp[64:128, h * 128:(h + 1) * 128],
                           iv[j][64:128, h * 128:(h + 1) * 128], True, True) for h in range(4)],
                 R=[kpb, ivb[j]], W=[pb_m1])
            free(kpb)
            eblv = ebl[:, :].rearrange("p (h c) -> p h c", c=2)
            e01, e01b = rot('s8')
            C.op('dve', TT(e01[:, 0:4].unsqueeze(2), eblv[:, :, 0:1], eblv[:, :, 1:2], ALU.mult), R=[eblb], W=[e01b])
            yield
            c0, c0b = rot('f512')
            C.op('dve', TT(hv(c0[:, :]), hv(ps_m0[:, :]), e01[:, 0:4].unsqueeze(2).to_broadcast([128, 4, 128]), ALU.mult),
                 R=[pb_m0, e01b], W=[c0b])
            c1, c1b = rot('f512')
            C.op('dve', TT(hv(c1[:, :]), hv(ps_m1[:, :]), eblv[:, :, 1:2].to_broadcast([128, 4, 128]), ALU.mult),
                 R=[pb_m1, eblb], W=[c1b])
            free(pb_m1)
            C.op('pool', TT(c0[:, :], c0[:, :], c1[:, :], ALU.add), R=[c0b, c1b], W=[c0b])
            free(c1b)
            yield
            while j > 0 and not flags.get(('st', j - 1)):
                yield
            Sflat = Sst[l][:, :, :].rearrange("p h v -> p (h v)")
            stmp, stmpb = rot('f512')
            C.op('dve', TT(stmp[:, :], ps_m0[:, :], Sflat, ALU.add), R=[pb_m0, Sstb[l]], W=[stmpb])
            free(pb_m0)
            S1bf_, S1bfb = rot('b512')
            S1bf = hv(S1bf_[:, :])
            C.op('dve', TT(S1bf, hv(stmp[:, :]), eblv[:, :, 0:1].to_broadcast([128, 4, 128]), ALU.mult),
                 R=[stmpb, eblb], W=[S1bfb])
            free(stmpb)
            C.op('dve', TT(Sst[l][:, :, :], Sst[l][:, :, :], e01[:, 0:4].unsqueeze(2).to_broadcast([128, 4, 128]), ALU.mult),
                 R=[Sstb[l], e01b], W=[Sstb[l]])
            C.op('dve', TT(Sflat, Sflat, c0[:, :], ALU.add), R=[Sstb[l], c0b], W=[Sstb[l]])
            free(c0b, e01b, eblb)
            while j > 0 and not flags.get(('o', j - 1)):
                yield
            C.op('act', ACT(Sbf[l][(j + 1) % 2][:, :, :], Sst[l][:, :, :], AF.Copy), R=[Sstb[l]], W=[Sbfb[l][(j + 1) % 2]])
            flags[('st', j)] = True
            yield
            ps_o, pb_o = yield from gbank()
            fns = []
            for h in range(4):
                fns.append(MM(ps_o[:, h * 128:(h + 1) * 128], attm[:, h * 128:(h + 1) * 128], iv[j][:, h * 128:(h + 1) * 128], True, False))
                fns.append(MM(ps_o[0:64, h * 128:(h + 1) * 128], qkT[:, h * 128:h * 128 + 64], Sbf[l][j % 2][:, h, :], False, False))
                fns.append(MM(ps_o[64:128, h * 128:(h + 1) * 128], qkT[:, h * 128 + 64:h * 128 + 128], S1bf[:, h, :], False, True))
            C.op('pe', fns, R=[attmb, ivb[j], qkTb, Sbfb[l][j % 2], S1bfb], W=[pb_o])
            free(attmb, qkTb, S1bfb)
            flags[('o', j)] = True
            yield
            osb, osbb = rot('f512')
            C.op('act', ACT(osb[:, :], ps_o[:, :], AF.Copy), R=[pb_o], W=[osbb])
            free(pb_o)
            og, ogb = rot('f512')
            C.op('dve', TT(og[:, :], osb[:, :], gsil[j][:, :], ALU.mult), R=[osbb, gsilb[j]], W=[ogb])
            sqt, sqtb = rot('f512')
            C.op('dve', TT(sqt[:, :], osb[:, :], osb[:, :], ALU.mult), R=[osbb], W=[sqtb])
            free(osbb)
            ssq4, ssq4b = rot('s8')
            C.op('dve', RED(ssq4[:, 0:4], sqt[:, :].rearrange("p (h d) -> p h d", h=4)), R=[sqtb], W=[ssq4b])
            free(sqtb)
            rs4, rs4b = rsq_small((ssq4, ssq4b), 4, 1.0 / 128, EPS, 'rs4')
            rout, routb = rot('b512')
            C.op('dve', TT(rout[:, :].rearrange("p (h d) -> p h d", h=4), og[:, :].rearrange("p (h d) -> p h d", h=4),
                           rs4[:, 0:4].unsqueeze(2).to_broadcast([128, 4, 128]), ALU.mult), R=[ogb, rs4b], W=[routb])
            free(ogb)
            yield
            ps, pb = yield from gbank()
            psv = ps[:, :].bitcast(BF16)
            C.op('pe', [TR(psv[:, c * 128:(c + 1) * 128], rout[:, c * 128:(c + 1) * 128], ident[:, :]) for c in range(4)],
                 R=[routb, ident_b], W=[pb])
            C.op('act', ACT(rT[:, :, j * 128:(j + 1) * 128], psv[:, 0:512].rearrange("p (c t) -> p c t", c=4), AF.Copy),
                 R=[pb], W=[rTb[j]])
            if l == 0 and j == 1:
                dump('rout', rout[:, :], [routb])

        def merge(l):
            for t in range(2):
                wa, wab = w_acquire(l, 10 + 2 * t)
                wh, whb = w_acquire(l, 11 + 2 * t)
                for ii in range(4):
                    i = t * 4 + ii
                    psa, pba = bank()
                    C.op('pe', [MM(psa[:, :], wa[:, k, ii * 128:(ii + 1) * 128], aT[:, k, :], k == 0, k == 3) for k in range(4)],
                         R=aTb + [wab], W=[pba])
                    psh, pbh = bank()
                    C.op('pe', [MM(psh[:, :], wh[:, k, ii * 128:(ii + 1) * 128], rT[:, k, :], k == 0, k == 3) for k in range(4)],
                         R=rTb + [whb], W=[pbh])
                    t1, t1b = rot('f512')
                    C.op('dve', STT(t1[:, :], sigT[:, i, :], 1.0, psa[:, :], ALU.add, ALU.mult), R=[pba, sigTb[i]], W=[t1b])
                    t2, t2b = rot('f512')
                    C.op('dve', STT(t2[:, :], sigT[:, 8 + i, :], 1.0, psh[:, :], ALU.add, ALU.mult), R=[pbh, sigTb[8 + i]], W=[t2b])
                    C.op('dve' if i % 2 == 1 else 'pool', TT(hT[:, i, :], t1[:, :], t2[:, :], ALU.add), R=[t1b, t2b], W=hTb)
                    flush()
                w_load()
                w_load()

        def out_proj_g(l, key, xi):
            xs, xb = XS[xi], XB[xi]
            tiles = [w_acquire(l, 14 + n) for n in range(2)]
            for j in range(NSUB):
                for n in range(2):
                    wt, wb = tiles[n]
                    ps, pb = yield from gbank()
                    C.op('pe', [MM(ps[:, :], hT[:, k, j * 128:(j + 1) * 128], wt[:, k, :], k == 0, k == 7) for k in range(8)],
                         R=[hTb[j], wb], W=[pb])
                    xv = xs[:, j, n * 512:(n + 1) * 512]
                    C.op('dve', STT(xv, ps[:, :], 0.5, xv, ALU.mult, ALU.add), R=[pb, xb[j]], W=[xb[j]])
                    free(pb)
                xflags[(key, j)] = True
                yield
            w_load()
            w_load()

        def ffn_up(l, first):
            for t in range(11):
                wt, wb = w_acquire(l, 16 + t)
                for cc in range(2):
                    c = 2 * t + cc
                    ys = []
                    for (col0, cidx) in ((cc * 128, c), (256 + cc * 128, 22 + c)):
                        ps, pb = bank()
                        C.op('pe', [MM(ps[:, :], wt[:, k, col0:col0 + 128], hT[:, k, :], k == 0, k == 7) for k in range(8)],
                             R=hTb + [wb], W=[pb])
                        uc, ucb = rot('ucat')
                        C.op('act', ACT(uc[:, 2:514], ps[:, :], AF.Copy), R=[pb], W=[ucb])
                        y, yb = rot('f512')
                        C.op('act', ACT(y[:, :], ps[:, :], AF.Identity, scale=cw[l][:, 2, cidx:cidx + 1],
                                        bias=cw[l][:, 3, cidx:cidx + 1]), R=[pb, cw_b[l]], W=[yb])
                        if first:
                            C.op('pool', MS(uc[:, 0:2], 0.0), W=[ucb])
                        else:
                            C.op('pool', CP(uc[:, 0:2], ccar[l][:, cidx, :]), R=[ccarb[l][cidx]], W=[ucb])
                        C.op('dve', STT(y[:, :], uc[:, 1:513], cw[l][:, 1, cidx:cidx + 1], y[:, :], ALU.mult, ALU.add),
                             R=[ucb, yb, cw_b[l]], W=[yb])
                        C.op('dve', STT(y[:, :], uc[:, 0:512], cw[l][:, 0, cidx:cidx + 1], y[:, :], ALU.mult, ALU.add),
                             R=[ucb, yb, cw_b[l]], W=[yb])
                        C.op('pool', CP(ccar[l][:, cidx, :], uc[:, 512:514]), R=[ucb], W=[ccarb[l][cidx]])
                        ys.append((y, yb))
                    sg, sgb = rot('f512')
                    C.op('act', ACT(sg[:, :], ys[0][0][:, :], AF.Silu), R=[ys[0][1]], W=[sgb])
                    C.op('dve', TT(actT_c[c], ys[1][0][:, :], sg[:, :], ALU.mult), R=[ys[1][1], sgb], W=[actTb[c]])
                    flush()
                w_load()

        def ffn_down_g(l, key, after_j, xi):
            xs, xb = XS[xi], XB[xi]
            n = 0
            bks = []
            for _ in range(NSUB):
                bk_ = yield from gbank()
                bks.append(bk_)
            for kg in range(3):
                kc = 8 if kg < 2 else 6
                wt, wb = w_acquire(l, 27 + n * 3 + kg)
                for j in range(NSUB):
                    ps, pb = bks[j]
                    C.op('pe', [MM(ps[:, :], actT_c[kg * 8 + k][:, j * 128:(j + 1) * 128], wt[:, k, :],
                                   kg == 0 and k == 0, kg == 2 and k == kc - 1) for k in range(kc)],
                         R=[actTb[kg * 8 + k] for k in range(kc)] + [wb], W=[pb], selfdep=(kg == 0))
                w_load()
            for j in range(NSUB):
                ps, pb = bks[j]
                xv = xs[:, j, n * 512:(n + 1) * 512]
                C.op('dve', TT(xv, ps[:, :], xv, ALU.add), R=[pb, xb[j]], W=[xb[j]])
                free(pb)
            n = 1
            tiles = [w_acquire(l, 27 + n * 3 + kg) for kg in range(3)]
            for j in range(NSUB):
                ps, pb = yield from gbank()
                fns = []
                Rl = []
                for kg in range(3):
                    kc = 8 if kg < 2 else 6
                    wt, wb = tiles[kg]
                    for k in range(kc):
                        fns.append(MM(ps[:, :], actT_c[kg * 8 + k][:, j * 128:(j + 1) * 128], wt[:, k, :],
                                      kg == 0 and k == 0, kg == 2 and k == kc - 1))
                    Rl += [actTb[kg * 8 + k] for k in range(kc)] + [wb]
                C.op('pe', fns, R=Rl, W=[pb])
                xv = xs[:, j, n * 512:(n + 1) * 512]
                C.op('dve', TT(xv, ps[:, :], xv, ALU.add), R=[pb, xb[j]], W=[xb[j]])
                free(pb)
                if after_j is not None:
                    after_j(j)
                xflags[(key, j)] = True
                yield
            for _ in range(3):
                w_load()

        for _ in range(NSLOT):
            w_load()
        if NST > 1:
            load_x(1)
        order = [(st, l) for st in range(NST) for l in range(2)]
        for idx, (st, l) in enumerate(order):
            seq, pos = st // nst_seq, (st % nst_seq) * T
            first = (pos == 0)
            xi = st % 2
            if idx == 0:
                norm_T(g1T[l], g1T_b[l], xi)
            if first:
                C.op('pool', MS(Sst[l][:, :, :], 0.0), W=[Sstb[l]])
                C.op('pool', MS(Sbf[l][0][:, :, :], 0.0), W=[Sbfb[l][0]])
            proj(l)
            if idx == 0:
                build_bias()
            flags.clear()
            A_ = [attention(l, j, not (first and j == 0)) for j in range(NSUB)]
            H_ = [hgrn(l, j) for j in range(NSUB)]
            pend = [H_[0], A_[0], H_[1], H_[2], A_[1], H_[3], A_[2], A_[3]]
            windowed(pend, 4, always=[gates(l)], always_every=1, always_start=FILL_START)
            merge(l)
            xflags.clear()
            windowed([out_proj_g(l, 'x2', xi)] + [norm_T_j(g2T[l], g2T_b[l], j, 'x2', xi) for j in range(NSUB)], 5)
            if st == 0 and l == 0:
                dump('x1', XS[xi][:, 0, :], [XB[xi][0]])
            ffn_up(l, first)
            nxt = order[idx + 1] if idx + 1 < len(order) else None
            after = None
            if l == 1:
                def after(j, st=st, seq=seq, pos=pos, xi=xi):
                    C.dma('sp', out=y_d[seq, pos + j * 128:pos + (j + 1) * 128, :], in_=XS[xi][:, j, :], R=[XB[xi][j]],
                          sem='xst%d_%d' % (xi, j))
                    if j == NSUB - 1 and st + 2 < NST:
                        load_x(st + 2)
            gens = [ffn_down_g(l, 'x1', after, xi)]
            if nxt is not None:
                ln = nxt[1]
                if l == 1:
                    for j in range(NSUB):
                        xflags[('x1', j)] = True
                    gens = [norm_T_j(g1T[ln], g1T_b[ln], j, 'x1', 1 - xi) for j in range(NSUB)] + gens
                else:
                    gens += [norm_T_j(g1T[ln], g1T_b[ln], j, 'x1', xi) for j in range(NSUB)]
            windowed(gens, 5)
        C.final_wait('sp', ['xst%d_%d' % (a, b) for a in range(2) for b in range(NSUB)] + ['dbg'])
        print('tracker: waits emitted', C.nwait)
        C.emit()
    return nc


_W_NAMES = ["norm1", "w_in", "q_norm", "k_norm", "sinks", "rel_bias", "hg_lb", "hg_norm", "w_pa", "w_ph",
            "w_out", "norm2", "w_up", "conv_w", "conv_b", "w_down"]


def kernel(**inputs):
    x = np.ascontiguousarray(np.asarray(inputs["x"], dtype=np.float32))
    Bt, S, _ = x.shape
    ncores = 8
    nseq = Bt // ncores
    nc = build(nseq, S)
    wts = {k: np.ascontiguousarray(np.asarray(inputs[k], dtype=np.float32)) for k in _W_NAMES}
    in_maps = []
    for c in range(ncores):
        m = {"x": x[c * nseq:(c + 1) * nseq]}
        m.update(wts)
        in_maps.append(m)
    res = run_bass_kernel_spmd(nc, in_maps, core_ids=list(range(ncores)))
    return np.concatenate([r["y"] for r in res.results], axis=0)
```
